# Optimizing a Trainium2 kernel written in Bass

```python
import jax, jax.numpy as jnp
from jax import lax
import numpy as np

D_MODEL = 1024
BATCH = 16
SEQ = 2048
DEPTH = 2

GRID_W = 64
CTX_LEN = 256
EPS = 1e-6
RET_HEADS = 4
RET_DK = 64
RET_DV = 128
RET_CHUNK = 128
RET_DECAY_EXP = (5.0, 7.0, 9.0, 11.0)
RET_ROPE_BASE = 10000.0
POOL_GROUPS = 4
POOL_GW = 128
POOL_WINDOWS = (2, 4, 8, 16)
ATT_HEADS = 8
ATT_KV_HEADS = 2
ATT_HD = 64
ATT_WINDOW = 128
ATT_BLOCK = 128
ROPE_BASE = 10000.0
MLP_HIDDEN = 4 * D_MODEL
N_BRANCH = 3

RET_QK_W = RET_HEADS * RET_DK
RET_V_W = RET_HEADS * RET_DV
ATT_Q_W = ATT_HEADS * ATT_HD
ATT_KV_W = ATT_KV_HEADS * ATT_HD
POOL_W = POOL_GROUPS * POOL_GW
CTX_SIDE_SIZES = (RET_QK_W, RET_V_W, ATT_KV_W, ATT_KV_W)
QUERY_SIDE_SIZES = (RET_QK_W, RET_V_W, ATT_Q_W, POOL_W, N_BRANCH * D_MODEL)
IN_SIZES = CTX_SIDE_SIZES + QUERY_SIDE_SIZES
CTX_SIDE_COLS = sum(CTX_SIDE_SIZES)
IN_COLS = sum(IN_SIZES)

kernel_name = 'hybrid_retention_pool_swa_dit_block'

F32 = jnp.float32


def rmsnorm(x, w=None):
    xf = x.astype(F32)
    y = xf * lax.rsqrt(jnp.mean(xf * xf, axis=-1, keepdims=True) + EPS)
    if w is not None:
        y = y * w.astype(F32)
    return y.astype(x.dtype)


def modulate(h, shift, scale):
    return h * (1 + scale) + shift


def split_cols(z, sizes):
    out, start = [], 0
    for s in sizes:
        out.append(z[..., start:start + s])
        start += s
    return out


def heads(t, n):
    return t.reshape(t.shape[0], t.shape[1], n, -1)


def rope(x, pos, base):
    half = x.shape[-1] // 2
    freq = base ** (-jnp.arange(half, dtype=F32) / half)
    ang = pos.astype(F32)[:, None] * freq[None, :]
    cos, sin = jnp.cos(ang)[:, None, :], jnp.sin(ang)[:, None, :]
    x1, x2 = x[..., :half].astype(F32), x[..., half:].astype(F32)
    return jnp.concatenate([x1 * cos - x2 * sin, x1 * sin + x2 * cos], axis=-1).astype(x.dtype)


def axial_rope(x, row, col):
    h = x.shape[-1] // 2
    return jnp.concatenate([rope(x[..., :h], row, ROPE_BASE), rope(x[..., h:], col, ROPE_BASE)], axis=-1)


def retention_scan(q, k, v, log_gamma, r0):
    B, L, H, dk = q.shape
    dv = v.shape[-1]
    C = RET_CHUNK
    n = L // C
    qc = q.reshape(B, n, C, H, dk)
    kc = k.reshape(B, n, C, H, dk)
    vc = v.reshape(B, n, C, H, dv)
    idx = jnp.arange(C, dtype=F32)
    rel = idx[:, None] - idx[None, :]
    decay = jnp.where(rel[None] >= 0, jnp.exp(jnp.maximum(rel, 0.0)[None] * log_gamma[:, None, None]), 0.0).astype(q.dtype)
    scores = jnp.einsum('bnihd,bnjhd->bnhij', qc, kc) * decay
    y_inner = jnp.einsum('bnhij,bnjhe->bnihe', scores, vc)
    zeta = jnp.exp((C - 1 - idx)[:, None] * log_gamma[None, :]).astype(q.dtype)
    u = jnp.einsum('bnjhd,jh,bnjhe->nbhde', kc, zeta, vc)
    chunk_decay = jnp.exp(C * log_gamma).astype(q.dtype)[:, None, None]

    def step(r, u_i):
        return chunk_decay * r + u_i, r

    r_final, r_prev = lax.scan(step, r0, u)
    xi = jnp.exp((idx + 1)[:, None] * log_gamma[None, :]).astype(q.dtype)
    y_cross = jnp.einsum('bnihd,ih,nbhde->bnihe', qc, xi, r_prev)
    return (y_inner + y_cross).reshape(B, L, H, dv), r_final


def retention_final_state(k, v, log_gamma):
    L = k.shape[1]
    w = jnp.exp((L - 1 - jnp.arange(L, dtype=F32))[:, None] * log_gamma[None, :]).astype(k.dtype)
    return jnp.einsum('blhd,lh,blhe->bhde', k, w, v)


def bidir_retention(q, k, v, log_gamma, r0_f, r0_b):
    y_f, r_f = retention_scan(q, k, v, log_gamma[0], r0_f)
    y_b, r_b = retention_scan(q[:, ::-1], k[:, ::-1], v[:, ::-1], log_gamma[1], r0_b)
    return y_f + y_b[:, ::-1], r_f, r_b


def retention_output(y, g):
    B, L = y.shape[:2]
    return jax.nn.silu(g) * rmsnorm(y).reshape(B, L, RET_V_W)


def multiscale_pool(u, w_grp, scale):
    B, L, _ = u.shape
    ug = u.reshape(B, L, POOL_GROUPS, POOL_GW)
    cs = jnp.cumsum(ug.astype(F32), axis=1)
    cs = jnp.concatenate([jnp.zeros_like(cs[:, :1]), cs], axis=1)
    t = jnp.arange(L)
    means = []
    for g, w in enumerate(POOL_WINDOWS):
        lo = jnp.clip(t - w // 2, 0, L)
        hi = jnp.clip(t + w // 2, 0, L)
        csg = cs[:, :, g]
        means.append((csg[:, hi] - csg[:, lo]) / (hi - lo).astype(F32)[:, None])
    pooled = jnp.stack(means, axis=2).astype(u.dtype)
    mixed = jnp.einsum('blgc,gcd->blgd', pooled - ug, w_grp)
    return mixed.reshape(B, L, POOL_W) * scale


def softmax_with_sink(logits, sink):
    lf = logits.astype(F32)
    s = jnp.broadcast_to(sink.astype(F32).reshape(ATT_KV_HEADS, -1, 1, 1), lf.shape[:-1] + (1,))
    p = jax.nn.softmax(jnp.concatenate([lf, s], axis=-1), axis=-1)
    return p[..., :-1]


def context_attention(q, k, v, sink):
    B, L, H, dh = q.shape
    G = H // ATT_KV_HEADS
    qg = q.reshape(B, L, ATT_KV_HEADS, G, dh)
    s = jnp.einsum('bikgd,bjkd->bkgij', qg, k) * dh ** -0.5
    p = softmax_with_sink(s, sink).astype(v.dtype)
    return jnp.einsum('bkgij,bjkd->bikgd', p, v).reshape(B, L, H * dh)


def windowed_attention(q, k, v, kc, vc, sink):
    B, S, H, dh = q.shape
    nb = S // ATT_BLOCK
    G = H // ATT_KV_HEADS
    scale = dh ** -0.5
    qb = q.reshape(B, nb, ATT_BLOCK, ATT_KV_HEADS, G, dh)

    def band(t):
        tb = t.reshape(B, nb, ATT_BLOCK, ATT_KV_HEADS, dh)
        tp = jnp.pad(tb, ((0, 0), (1, 1), (0, 0), (0, 0), (0, 0)))
        return jnp.concatenate([tp[:, :-2], tp[:, 1:-1], tp[:, 2:]], axis=2)

    kb, vb = band(k), band(v)
    blk = jnp.arange(nb)[:, None]
    qpos = blk * ATT_BLOCK + jnp.arange(ATT_BLOCK)[None, :]
    kpos = (blk - 1) * ATT_BLOCK + jnp.arange(3 * ATT_BLOCK)[None, :]
    valid = ((jnp.abs(qpos[:, :, None] - kpos[:, None, :]) <= ATT_WINDOW)
             & (kpos[:, None, :] >= 0) & (kpos[:, None, :] < S))
    s_loc = jnp.einsum('bnikgd,bnjkd->bnkgij', qb, kb) * scale
    s_loc = jnp.where(valid[None, :, None, None], s_loc, -jnp.inf)
    s_ctx = jnp.einsum('bnikgd,bjkd->bnkgij', qb, kc) * scale
    p = softmax_with_sink(jnp.concatenate([s_loc, s_ctx], axis=-1), sink).astype(v.dtype)
    nl = 3 * ATT_BLOCK
    o = (jnp.einsum('bnkgij,bnjkd->bnikgd', p[..., :nl], vb)
         + jnp.einsum('bnkgij,bjkd->bnikgd', p[..., nl:], vc))
    return o.reshape(B, S, H * dh)


def merge_branches(y_ret, y_pool, y_att, gates, w_ret_out, w_pool_out, w_attn_out, w_out):
    g_r, g_p, g_a = split_cols(jax.nn.sigmoid(gates), (D_MODEL,) * N_BRANCH)
    y = g_r * (y_ret @ w_ret_out) + g_p * (y_pool @ w_pool_out) + g_a * (y_att @ w_attn_out)
    return y @ w_out


def sq_relu_mlp(h, w1, w2):
    return jnp.square(jax.nn.relu(h @ w1)) @ w2


def setup_inputs(seed: int = 0) -> dict:
    key = jax.random.key(seed)
    ks = jax.random.split(key, 21)

    def nrm(k, shape, fan_in):
        return jax.random.normal(k, shape, F32) * fan_in ** -0.5

    def near_one(k, shape):
        return 1.0 + 0.02 * jax.random.normal(k, shape, F32)

    return {
        'x': jax.random.normal(ks[0], (BATCH, SEQ, D_MODEL), F32),
        'c': jax.random.normal(ks[1], (BATCH, D_MODEL), F32),
        'ctx': jax.random.normal(ks[2], (BATCH, CTX_LEN, D_MODEL), F32),
        'c_ctx': jax.random.normal(ks[3], (D_MODEL,), F32),
        'norm1_w': near_one(ks[4], (DEPTH, D_MODEL)),
        'norm2_w': near_one(ks[5], (DEPTH, D_MODEL)),
        'ada_w': nrm(ks[6], (DEPTH, D_MODEL, 6 * D_MODEL), D_MODEL),
        'ada_b': 0.02 * jax.random.normal(ks[7], (DEPTH, 6 * D_MODEL), F32),
        'w_in': nrm(ks[8], (DEPTH, D_MODEL, IN_COLS), D_MODEL),
        'ret_decay': jnp.asarray(RET_DECAY_EXP, F32)[None, None, :]
                     + 0.1 * jax.random.normal(ks[9], (DEPTH, 2, RET_HEADS), F32),
        'pool_w': nrm(ks[10], (DEPTH, POOL_GROUPS, POOL_GW, POOL_GW), POOL_GW),
        'pool_scale': near_one(ks[11], (DEPTH, POOL_W)),
        'q_norm_w': near_one(ks[12], (DEPTH, ATT_HD)),
        'k_norm_w': near_one(ks[13], (DEPTH, ATT_HD)),
        'attn_sink': 0.5 * jax.random.normal(ks[14], (DEPTH, ATT_HEADS), F32),
        'w_ret_out': nrm(ks[15], (DEPTH, RET_V_W, D_MODEL), RET_V_W),
        'w_pool_out': nrm(ks[16], (DEPTH, POOL_W, D_MODEL), POOL_W),
        'w_attn_out': nrm(ks[17], (DEPTH, ATT_Q_W, D_MODEL), ATT_Q_W),
        'w_out': nrm(ks[18], (DEPTH, D_MODEL, D_MODEL), D_MODEL),
        'w_mlp1': nrm(ks[19], (DEPTH, D_MODEL, MLP_HIDDEN), D_MODEL),
        'w_mlp2': nrm(ks[20], (DEPTH, MLP_HIDDEN, D_MODEL), MLP_HIDDEN),
    }


def reference(x, c, ctx, c_ctx, norm1_w, norm2_w, ada_w, ada_b, w_in, ret_decay, pool_w, pool_scale,
              q_norm_w, k_norm_w, attn_sink, w_ret_out, w_pool_out, w_attn_out, w_out, w_mlp1, w_mlp2):
    B, S, _ = x.shape
    ROWS = S // GRID_W
    row = jnp.repeat(jnp.arange(ROWS), GRID_W)
    col = jnp.tile(jnp.arange(GRID_W), ROWS)
    seq_pos = jnp.arange(S)
    xc = ctx
    for l in range(DEPTH):
        last = l == DEPTH - 1
        log_gamma = jnp.log1p(-jnp.exp2(-ret_decay[l].astype(F32)))
        mod_x = jax.nn.silu(c) @ ada_w[l] + ada_b[l]
        sh1, sc1, g1, sh2, sc2, g2 = [m[:, None, :] for m in split_cols(mod_x, (D_MODEL,) * 6)]
        mod_c = jax.nn.silu(c_ctx) @ ada_w[l] + ada_b[l]
        csh1, csc1, cg1, csh2, csc2, cg2 = split_cols(mod_c, (D_MODEL,) * 6)

        uc = modulate(rmsnorm(xc, norm1_w[l]), csh1, csc1)
        if last:
            rk_c, rv_c, ak_c, av_c = split_cols(uc @ w_in[l][:, :CTX_SIDE_COLS], CTX_SIDE_SIZES)
        else:
            rk_c, rv_c, ak_c, av_c, rq_c, rg_c, aq_c, pu_c, gt_c = split_cols(uc @ w_in[l], IN_SIZES)
        rk_c = heads(rk_c, RET_HEADS) * RET_DK ** -0.5
        rv_c = heads(rv_c, RET_HEADS)
        ak_c = rmsnorm(heads(ak_c, ATT_KV_HEADS), k_norm_w[l])
        av_c = heads(av_c, ATT_KV_HEADS)
        if last:
            r_f = retention_final_state(rk_c, rv_c, log_gamma[0])
            r_b = retention_final_state(rk_c[:, ::-1], rv_c[:, ::-1], log_gamma[1])
        else:
            zero = jnp.zeros((B, RET_HEADS, RET_DK, RET_DV), rv_c.dtype)
            yr_c, r_f, r_b = bidir_retention(heads(rq_c, RET_HEADS), rk_c, rv_c, log_gamma, zero, zero)
            ya_c = context_attention(rmsnorm(heads(aq_c, ATT_HEADS), q_norm_w[l]), ak_c, av_c, attn_sink[l])
            yp_c = multiscale_pool(pu_c, pool_w[l], pool_scale[l])
            mix_c = merge_branches(retention_output(yr_c, rg_c), yp_c, ya_c, gt_c,
                                   w_ret_out[l], w_pool_out[l], w_attn_out[l], w_out[l])

        ux = modulate(rmsnorm(x, norm1_w[l]), sh1, sc1)
        rk, rv, ak, av, rq, rg, aq, pu, gt = split_cols(ux @ w_in[l], IN_SIZES)
        rq = rope(heads(rq, RET_HEADS), seq_pos, RET_ROPE_BASE)
        rk = rope(heads(rk, RET_HEADS), seq_pos, RET_ROPE_BASE) * RET_DK ** -0.5
        yr, _, _ = bidir_retention(rq, rk, heads(rv, RET_HEADS), log_gamma, r_f, r_b)
        aq = axial_rope(rmsnorm(heads(aq, ATT_HEADS), q_norm_w[l]), row, col)
        ak = axial_rope(rmsnorm(heads(ak, ATT_KV_HEADS), k_norm_w[l]), row, col)
        ya = windowed_attention(aq, ak, heads(av, ATT_KV_HEADS), ak_c, av_c, attn_sink[l])
        yp = multiscale_pool(pu, pool_w[l], pool_scale[l])
        mix = merge_branches(retention_output(yr, rg), yp, ya, gt,
                             w_ret_out[l], w_pool_out[l], w_attn_out[l], w_out[l])
        x = x + g1 * mix
        x = x + g2 * sq_relu_mlp(modulate(rmsnorm(x, norm2_w[l]), sh2, sc2), w_mlp1[l], w_mlp2[l])

        if not last:
            xc = xc + cg1 * mix_c
            xc = xc + cg2 * sq_relu_mlp(modulate(rmsnorm(xc, norm2_w[l]), csh2, csc2), w_mlp1[l], w_mlp2[l])
    return x
```

```python
import os
import numpy as np
from contextlib import ExitStack
CUT0 = float(os.environ.get("KCUT", "99"))
KSEQ = os.environ.get("KSEQ", "")
CUT = 99.0
import concourse.bass as bass
import concourse.mybir as mybir
from concourse.bass_utils import run_bass_kernel_spmd

F32 = mybir.dt.float32
BF16 = mybir.dt.bfloat16
AF = mybir.ActivationFunctionType
ALU = mybir.AluOpType

NCORES = 8
NB = 2
D = 1024
KC = 8
S = 2048
LC = 256
DEPTH = 2
TL = 512
EPS = 1e-6
NEG = -30000.0
SAME_ENGINE_SYNC = True

C_RK, C_RV, C_AK, C_AV = 0, 256, 768, 896
C_RQ, C_RG, C_AQ, C_PU, C_GT = 1024, 1280, 1792, 2304, 2816

CF_RELP, CF_RELN, CF_IOTA1, CF_IOTAR, CF_COLJ, CF_COL127, CF_CORRL, CF_CORRR, CF_W = 0, 128, 256, 384, 512, 513, 514, 546, 584
(CB_ONES1024, CB_ONES128, CB_BD64, CB_IDENT, CB_PMR, CB_PMA, CB_SELA0, CB_SELA1, CB_SELB0, CB_SELB1,
 CB_SWA0, CB_SWA1, CB_SWB0, CB_SWB1, CB_MASKP, CB_MASKN) = [i * 128 for i in range(16)]
CB_IDENT2 = 16 * 128
CB_ONES = CB_IDENT2 + 512
CB_W = CB_ONES + 64


def _host_consts():
    j = np.arange(128, dtype=np.float64)[:, None]
    i = np.arange(128, dtype=np.float64)[None, :]
    cf = np.zeros((128, CF_W), np.float32)
    BIG = 1.0e7
    cf[:, CF_RELP:CF_RELP + 128] = np.where(i >= j, i - j, BIG)
    cf[:, CF_RELN:CF_RELN + 128] = np.where(j >= i, j - i, BIG)
    cf[:, CF_IOTA1:CF_IOTA1 + 128] = i + 1
    cf[:, CF_IOTAR:CF_IOTAR + 128] = 128 - i
    cf[:, CF_COLJ] = j[:, 0]
    cf[:, CF_COL127] = 127 - j[:, 0]
    wins = (2, 4, 8, 16)
    for g, w in enumerate(wins):
        for t in range(8):
            cntl = min(t + w // 2, 10 ** 9) - max(t - w // 2, 0)
            cf[:, CF_CORRL + g * 8 + t] = w / cntl
            tt = 7 - t
            hi = min((10 ** 6 - 1 - tt) + w // 2, 10 ** 6)
            lo = (10 ** 6 - 1 - tt) - w // 2
            cf[:, CF_CORRR + g * 8 + t] = w / (hi - lo)
    cb = np.zeros((128, CB_W), np.float32)
    cb[:, CB_ONES1024:CB_ONES1024 + 128] = 1.0 / 1024
    cb[:, CB_ONES128:CB_ONES128 + 128] = 1.0 / 128
    k = np.arange(128)[:, None]
    m = np.arange(128)[None, :]
    cb[:, CB_BD64:CB_BD64 + 128] = (k // 64 == m // 64) / 64.0
    cb[:, CB_IDENT:CB_IDENT + 128] = (k == m)
    swr = np.where(m % 64 < 32, m + 32, m - 32)
    cb[:, CB_PMR:CB_PMR + 128] = (k == swr)
    swa = np.where(m % 32 < 16, m + 16, m - 16)
    cb[:, CB_PMA:CB_PMA + 128] = (k == swa)
    for s in range(2):
        selA = (m < 64) & (k == 64 * s + m)
        selB = (m >= 64) & (k == 64 * s + (m - 64))
        mm = m % 64
        swm = np.where(mm % 32 < 16, mm + 16, mm - 16)
        swA = (m < 64) & (k == 64 * s + swm)
        swB = (m >= 64) & (k == 64 * s + swm)
        cb[:, CB_SELA0 + s * 128:CB_SELA0 + (s + 1) * 128] = selA
        cb[:, CB_SELB0 + s * 128:CB_SELB0 + (s + 1) * 128] = selB
        cb[:, CB_SWA0 + s * 128:CB_SWA0 + (s + 1) * 128] = swA
        cb[:, CB_SWB0 + s * 128:CB_SWB0 + (s + 1) * 128] = swB
    cb[:, CB_MASKP:CB_MASKP + 128] = np.where(m < k, NEG, 0.0)
    cb[:, CB_MASKN:CB_MASKN + 128] = np.where(m > k, NEG, 0.0)
    cb[:, CB_IDENT2:CB_IDENT2 + 512] = np.tile((k == m).astype(np.float32), (1, 4))
    cb[:, CB_ONES:CB_ONES + 64] = 1.0
    rope = np.zeros((4, 128, S), np.float32)
    p = np.arange(128)
    d = p % 64
    t = np.arange(S)
    fr = (np.float32(10000.0) ** (-np.arange(32, dtype=np.float32) / np.float32(32))).astype(np.float32)
    ang = (t.astype(np.float32)[None, :] * fr[d % 32][:, None]).astype(np.float32).astype(np.float64)
    rope[0] = np.cos(ang)
    rope[1] = np.sin(ang) * np.where(d < 32, -1.0, 1.0)[:, None]
    fa = (np.float32(10000.0) ** (-np.arange(16, dtype=np.float32) / np.float32(16))).astype(np.float32)
    row = (t // 64).astype(np.float32)
    col = (t % 64).astype(np.float32)
    pos = np.where((d < 32)[:, None], row[None, :], col[None, :]).astype(np.float32)
    anga = (pos * fa[d % 16][:, None]).astype(np.float32).astype(np.float64)
    rope[2] = np.cos(anga)
    rope[3] = np.sin(anga) * np.where(d % 32 < 16, -1.0, 1.0)[:, None]
    return cf, cb, rope


class DSem:
    def __init__(self, sem):
        self.sem = sem
        self.total = 0


class Buf:
    __slots__ = ("name", "w", "re", "rd", "const", "alias")

    def __init__(self, name, const=False):
        self.name = name
        self.w = None
        self.re = {}
        self.rd = {}
        self.const = const
        self.alias = []


class Op:
    __slots__ = ("eng", "fn", "edeps", "ddeps", "inc", "cnt", "dsem", "dval", "idx")


class Sched:
    ENG = ["pe", "act", "dve", "pool", "sp"]

    def __init__(self, nc, stack):
        self.nc = nc
        self.ops = {e: [] for e in self.ENG}
        self.esem = {e: stack.enter_context(nc.semaphore("es_" + e)) for e in self.ENG[:4]}
        self.stack = stack
        self.n = 0

    def dsem(self, name):
        return DSem(self.stack.enter_context(self.nc.semaphore(name)))

    @staticmethod
    def _adddep(ed, dd, dep):
        if dep is None:
            return
        if dep[0] == "e":
            o = dep[1]
            cur = ed.get(o.eng)
            if cur is None or cur.idx < o.idx:
                ed[o.eng] = o
        else:
            _, ds, v = dep
            if dd.get(ds, 0) < v:
                dd[ds] = v

    def add(self, eng, fn, reads=(), writes=(), dsem=None):
        o = Op()
        o.eng, o.fn, o.dsem, o.inc, o.cnt = eng, fn, dsem, False, None
        o.idx = self.n
        self.n += 1
        ed, dd = {}, {}
        for b in reads:
            self._adddep(ed, dd, b.w)
            for a in b.alias:
                self._adddep(ed, dd, a.w)
        for b in writes:
            for bb in [b] + b.alias:
                self._adddep(ed, dd, bb.w)
                for r in bb.re.values():
                    self._adddep(ed, dd, ("e", r))
                for ds, v in bb.rd.items():
                    self._adddep(ed, dd, ("d", ds, v))
        o.edeps, o.ddeps = ed, dd
        if dsem is not None:
            dsem.total += 16
            o.dval = dsem.total
            me = ("d", dsem, o.dval)
        else:
            me = ("e", o)
        for b in reads:
            if b.const:
                continue
            if dsem is not None:
                b.rd[dsem] = o.dval
            else:
                b.re[eng] = o
        for b in writes:
            b.w = me
            b.re = {}
            b.rd = {}
        self.ops[eng].append(o)
        return o

    def finalize(self, block):
        for e in self.ENG:
            for o in self.ops[e]:
                for pe_, p in list(o.edeps.items()):
                    if p.dsem is not None:
                        continue
                    if p.eng == o.eng:
                        if o.eng == "pe" or not SAME_ENGINE_SYNC or o.dsem is not None and False:
                            del o.edeps[pe_]
                            continue
                    p.inc = True
        for e in self.ENG[:4]:
            c = 0
            for o in self.ops[e]:
                if o.dsem is None and o.inc:
                    c += 1
                    o.cnt = c
        nc = self.nc

        def emit(engname, engobj):
            waited = {}
            for o in self.ops[engname]:
                for p in o.edeps.values():
                    sem = self.esem[p.eng]
                    if waited.get(sem.name, 0) < p.cnt:
                        engobj.wait_ge(sem, p.cnt)
                        waited[sem.name] = p.cnt
                for ds, v in o.ddeps.items():
                    if waited.get(ds.sem.name, 0) < v:
                        engobj.wait_ge(ds.sem, v)
                        waited[ds.sem.name] = v
                if o.fn is None:
                    continue
                ins = o.fn(engobj)
                if o.dsem is not None:
                    ins.then_inc(o.dsem.sem, 16)
                elif o.inc:
                    ins.then_inc(self.esem[engname], 1)

        block.tensor(lambda e: emit("pe", e))
        block.scalar(lambda e: emit("act", e))
        block.vector(lambda e: emit("dve", e))
        block.gpsimd(lambda e: emit("pool", e))
        block.sync(lambda e: emit("sp", e))


class Ring:
    def __init__(self, items):
        self.items = items
        self.i = 0

    def next(self):
        it = self.items[self.i % len(self.items)]
        self.i += 1
        return it


def bc_mid(ap, n):
    pairs = [list(x) for x in ap.ap]
    return bass.AP(ap.tensor, ap.offset, [pairs[0], [0, n]] + pairs[1:])


def bc_last(ap, n):
    pairs = [list(x) for x in ap.ap]
    return bass.AP(ap.tensor, ap.offset, pairs + [[0, n]])


class Prog:
    def __init__(self, dbg=None, stage=99):
        self.dbg = dbg or {}
        self.stage = stage
        self.dumps = []
        self.stack = ExitStack()
        self.nc = bass.Bass("TRN2", target_bir_lowering=False)
        self.build()

    def sb(self, name, shape, dt):
        return self.stack.enter_context(self.nc.sbuf_tensor(name, list(shape), dt))

    def din(self, name, shape, dt=F32):
        return self.nc.dram_tensor(name, list(shape), dt, kind="ExternalInput").ap()

    def mm(self, out, lhsT, rhs, start=True, stop=True, tp=None, R=(), W=()):
        kw = dict(start=start, stop=stop)
        if tp is not None:
            kw["tile_position"] = tp
        self.S.add("pe", lambda e, o=out, l=lhsT, r=rhs, kw=kw: e.matmul(o, l, r, **kw), R, W)

    def act(self, out, in_, func, bias=None, scale=None, R=(), W=()):
        kw = {}
        if bias is not None:
            kw["bias"] = bias
        if scale is not None:
            kw["scale"] = scale
        self.S.add("act", lambda e, o=out, i=in_, f=func, kw=kw: e.activation(out=o, in_=i, func=f, **kw), R, W)

    def tt(self, eng, out, in0, in1, op, R=(), W=()):
        self.S.add(eng, lambda e, o=out, a=in0, b=in1, op=op: e.tensor_tensor(out=o, in0=a, in1=b, op=op), R, W)

    def ts(self, eng, out, in0, s1, s2, op0, op1=None, R=(), W=()):
        if op1 is None:
            self.S.add(eng, lambda e, o=out, a=in0, s1=s1, op0=op0: e.tensor_scalar(o, a, s1, None, op0), R, W)
        else:
            self.S.add(eng, lambda e, o=out, a=in0, s1=s1, s2=s2, op0=op0, op1=op1: e.tensor_scalar(o, a, s1, s2, op0, op1), R, W)

    def stt(self, eng, out, in0, scalar, in1, op0, op1, R=(), W=()):
        self.S.add(eng, lambda e, o=out, a=in0, s=scalar, b=in1, op0=op0, op1=op1:
                   e.scalar_tensor_tensor(out=o, in0=a, scalar=s, in1=b, op0=op0, op1=op1), R, W)

    def cp(self, eng, out, in_, R=(), W=()):
        self.S.add(eng, lambda e, o=out, i=in_: e.tensor_copy(out=o, in_=i), R, W)

    def rsqrt(self, out, in_, R=(), W=()):
        self.act(out, in_, AF.Sqrt, bias=EPS, R=R, W=W)
        self.recip(out, out, R=W, W=W)

    def recip(self, out, in_, R=(), W=()):
        self.S.add("dve", lambda e, o=out, i=in_: e.reciprocal(out=o, in_=i), R, W)

    def dma(self, q, out, in_, dsem, R=(), W=(), slow=False):
        kw = {"allow_slow_non_contiguous": True} if slow else {}
        self.S.add(q, lambda e, o=out, i=in_, kw=kw: e.dma_start(out=o, in_=i, **kw), R, W, dsem=dsem)

    def dump(self, name, ap, bufs, shape, dt):
        d = self.nc.dram_tensor("dbg_" + name, list(shape), dt, kind="ExternalOutput").ap()
        self.dma("sp", d, ap, self.outds, R=list(bufs), W=[self.outb])
        self.dumps.append(name)

    def bank(self):
        return self.banks.next()

    def build(self):
        nc, st = self.nc, self.stack
        din = self.din
        self.xT = din("xT", [NB, D, S])
        self.cxT = din("cxT", [NB, D, LC])
        self.cT = din("cT", [D, 3])
        self.norm1_w = din("norm1_w", [DEPTH, D])
        self.norm2_w = din("norm2_w", [DEPTH, D])
        self.ada_w = din("ada_w", [DEPTH, D, 6 * D])
        self.ada_b = din("ada_b", [DEPTH, 6 * D])
        self.w_in = din("w_in", [DEPTH, D, 5888])
        self.ret_decay = din("ret_decay", [DEPTH * 2 * 4])
        self.pool_w = din("pool_w", [DEPTH, 4, 128, 128])
        self.pool_scale = din("pool_scale", [DEPTH, 512])
        self.q_norm_w = din("q_norm_w", [DEPTH, 64])
        self.k_norm_w = din("k_norm_w", [DEPTH, 64])
        self.attn_sink = din("attn_sink", [DEPTH * 8])
        self.w_ret_out = din("w_ret_out", [DEPTH, 512, D])
        self.w_pool_out = din("w_pool_out", [DEPTH, 512, D])
        self.w_attn_out = din("w_attn_out", [DEPTH, 512, D])
        self.w_out = din("w_out", [DEPTH, D, D])
        self.w_mlp1 = din("w_mlp1", [DEPTH, D, 4 * D])
        self.w_mlp2 = din("w_mlp2", [DEPTH, 4 * D, D])
        self.cstf_d = din("cstf", [128, CF_W])
        self.cstb_d = din("cstb", [128, CB_W])
        self.rope_d = din("rope", [4, 128, S])
        self.yT = nc.dram_tensor("yT", [NB, D, S], F32, kind="ExternalOutput").ap()
        self.NSLAB = 2 + 4 + 16 + 2 + 8 + 16
        self.wscr = nc.dram_tensor("wscr", [DEPTH, self.NSLAB, 128, 4096], BF16).ap()

        self.S = Sched(nc, st)
        S_ = self.S
        self.banks = Ring([])
        for i in range(8):
            t = st.enter_context(nc.psum_tensor("bank%d" % i, [128, 512], F32))
            self.banks.items.append((t, Buf("bank%d" % i)))
        sb = self.sb
        self.XT = sb("XT", [128, KC, S], F32)
        self.XTb = [Buf("XT%d" % i) for i in range(S // TL)]
        self.UX = sb("UX", [128, KC, TL], BF16)
        self.UXb = Buf("UX")
        self.RSTD = sb("RSTD", [128, TL], F32)
        self.RSTDb = Buf("RSTD")
        self.RFX = sb("RFX", [128, KC * LC], F32)
        self.RF = self.RFX.bitcast(BF16)[:].rearrange("p (n c e) -> p n c e", n=16, c=2)
        self.XC = self.RFX[:].rearrange("p (kc t) -> p kc t", kc=KC)
        self.RB = sb("RB", [128, 16, 2, 128], BF16)
        self.RFb = [Buf("RF%d" % i) for i in range(16)]
        self.RBb = [Buf("RB%d" % i) for i in range(16)]
        self.AKT = sb("AKT", [128, S], BF16)
        self.AKTb = [Buf("AKT%d" % i) for i in range(S // TL)]
        self.AV = sb("AV", [128, 16, 128], BF16)
        self.AVb = [Buf("AV%d" % i) for i in range(S // TL)]
        self.HALO = sb("HALO", [128, 4, 2, KC, 8], BF16)
        self.HALOb = [Buf("HALO%d" % i) for i in range(4)]
        self.ZHALO = sb("ZHALO", [128, KC, 8], BF16)
        self.ZHALOb = Buf("ZHALO", const=True)
        self.XCb = [Buf("XC")]
        self.XCb[0].alias = list(self.RFb)
        for b_ in self.RFb:
            b_.alias = list(self.XCb)
        self.RFc = sb("RFc", [128, 2, 2, 128], BF16)
        self.RBc = sb("RBc", [128, 2, 2, 128], BF16)
        self.RFcb = [Buf("RFc%d" % i) for i in range(2)]
        self.RBcb = [Buf("RBc%d" % i) for i in range(2)]
        self.AKTc = sb("AKTc", [128, DEPTH, LC], BF16)
        self.AKTcb = [Buf("AKTc%d" % l) for l in range(DEPTH)]
        self.AVc = sb("AVc", [128, DEPTH, 2, 128], BF16)
        self.AVcb = [Buf("AVc%d" % l) for l in range(DEPTH)]
        self.HALOc = sb("HALOc", [128, 1, 2, KC, 8], BF16)
        self.HALOcb = [Buf("HALOc")]
        self.R0 = sb("R0", [128, DEPTH, 2, 2, 128], F32)
        self.R0b = [[Buf("R0%d%d" % (l, d)) for d in range(2)] for l in range(DEPTH)]
        self.ST = sb("ST", [128, 2, 2, 2, 128], F32)
        self.STb = [[Buf("ST%d%d" % (d, i)) for i in range(2)] for d in range(2)]
        self.R1 = sb("R1", [128, 4096], BF16)
        self.KT = self.R1[:, 0:1024].rearrange("p (c t) -> p c t", c=2)
        self.VT = self.R1[:, 1024:3072].rearrange("p (c t) -> p c t", c=4)
        self.QT = self.R1[:, 3072:4096].rearrange("p (c t) -> p c t", c=2)
        self.HT = self.R1[:].rearrange("p (c t) -> p c t", c=8)
        self.QXF = sb("QXF", [128, 2, TL], BF16)
        self.QXB = sb("QXB", [128, 2, TL], BF16)
        self.AQT = sb("AQT", [128, 4, TL], BF16)
        self.KTb = [Buf("KT0"), Buf("KT1")]
        self.VTb = [Buf("VT%d" % i) for i in range(4)]
        self.QTb = [Buf("QT0"), Buf("QT1")]
        self.QXFb = [Buf("QXF0"), Buf("QXF1")]
        self.QXBb = [Buf("QXB0"), Buf("QXB1")]
        self.AQTb = [Buf("AQT%d" % i) for i in range(4)]
        self.HTb = [Buf("HT%d" % i) for i in range(8)]
        r1 = self.KTb + self.VTb + self.QTb
        for b_ in self.HTb:
            b_.alias = list(r1)
        for b_ in r1:
            b_.alias = list(self.HTb)
        self.YRT = sb("YRT", [128, 4, TL], BF16)
        self.YAT = sb("YAT", [128, 4, TL], BF16)
        self.YPT = sb("YPT", [128, 4, TL], BF16)
        self.YRTb = [Buf("YRT%d" % i) for i in range(4)]
        self.YATb = [Buf("YAT%d" % i) for i in range(4)]
        self.YPTb = [Buf("YPT%d" % i) for i in range(4)]
        self.R2 = sb("R2", [128, 4096], BF16)
        self.ZN = self.R2[:, 2048:4096].rearrange("p (c t) -> p c t", c=4)
        self.ZNb = [Buf("ZN%d" % i) for i in range(4)]
        self.SRG = self.R2[:, 0:2048].rearrange("p (c t) -> p c t", c=4)
        self.SRGb = [Buf("SRG%d" % i) for i in range(4)]
        self.YT = self.R2[:].rearrange("p (c t) -> p c t", c=8)
        self.YTb = [Buf("YT%d" % i) for i in range(KC)]
        r2 = self.SRGb + self.ZNb
        for b_ in self.YTb:
            b_.alias = list(r2)
        for b_ in r2:
            b_.alias = list(self.YTb)
        self.f32r = Ring([(sb("F32R%d" % i, [128, 528], F32), Buf("F32R%d" % i)) for i in range(6)])
        self.b16r = Ring([(sb("B16R%d" % i, [128, 512], BF16), Buf("B16R%d" % i)) for i in range(5)])
        self.ptr = Ring([(sb("PTR%d" % i, [128, 512], BF16), Buf("PTR%d" % i)) for i in range(5)])
        self.slabr = Ring([(sb("SLAB%d" % i, [128, 4096], BF16), Buf("SLAB%d" % i), S_.dsem("ds_slab%d" % i)) for i in range(2)])
        self.TAB = sb("TAB", [128, 2, TL], F32)
        self.TABb = Buf("TAB")
        self.TABds = S_.dsem("ds_tab")
        self.CF = sb("CF", [128, CF_W], F32)
        self.CB = sb("CB", [128, CB_W], BF16)
        self.CFb = Buf("CF", const=True)
        self.CBb = Buf("CB", const=True)
        self.PW = sb("PW", [128, DEPTH, 4, 128], BF16)
        self.PRM = sb("PRM", [128, 256], F32)
        self.PRMb = Buf("PRM", const=True)
        self.MODT = sb("MODT", [128, DEPTH, 48, 3], F32)
        self.MODA = sb("MODA", [128, DEPTH, 3, 16], F32)
        self.MODb = Buf("MOD", const=True)
        self.SCT = sb("SCT", [128, KC, 3], BF16)
        self.DT = sb("DT", [128, 4, 128], F32)
        self.XI = sb("XI", [128, 2, 2, 128], F32)
        self.ZETA = sb("ZETA", [128, 2, 4], F32)
        self.LTb = Buf("LT")
        self.cur_l = None
        self.cds = S_.dsem("ds_const")
        self.prepds = S_.dsem("ds_prep")
        self.xds = [S_.dsem("ds_x%d" % i) for i in range(S // TL)]
        self.xcds = S_.dsem("ds_xc")
        self.outds = S_.dsem("ds_out")
        self.wscrb = [Buf("wscr%d" % l) for l in range(DEPTH)]
        self.outb = Buf("out")
        self.dbgds = S_.dsem("ds_dbg")

        self.P_N1W, self.P_N2W = 0, 16
        self.P_ADAB = 32
        self.P_PSC = 128
        self.P_QNW, self.P_KNW = 136, 138
        self.P_SINK = 140
        self.P_LGB = 148
        self.P_LGC = 164
        self.P_GC = 172
        self.P_TMP = 180

        self.setup_consts()
        if self.stage >= 1:
            self.prep_weights()
        if self.stage >= 2:
            self.mod_vectors()
        if self.stage >= 3:
            for b in range(NB if self.stage >= 99 else 1):
                self.run_batch(b)
        if self.stage < 99:
            self.dump("PRM", self.PRM[:], [self.PRMb], [128, 256], F32)
            self.dump("MODT", self.MODT[:].rearrange("p a b c -> p (a b c)"), [self.MODb], [128, DEPTH * 48 * 3], F32)
        self.S.add("sp", None, reads=[self.outb])
        with nc.Block() as block:
            self.S.finalize(block)

    def prm(self, c0, n=1):
        return self.PRM[:, c0:c0 + n]

    def setup_consts(self):
        nc = self.nc
        cds = self.cds
        W = [self.PRMb]
        self.dma("sp", self.CF[:], self.cstf_d[:, :], cds, W=[self.CFb])
        self.dma("pool", self.CB[:], self.cstb_d[:, :], cds, W=[self.CBb])
        for l in range(DEPTH):
            self.dma("sp", self.PRM[:, self.P_N1W + l * 8:self.P_N1W + l * 8 + 8],
                     self.norm1_w[l].rearrange("(kc p) -> p kc", p=128), cds, W=W, slow=True)
            self.dma("sp", self.PRM[:, self.P_N2W + l * 8:self.P_N2W + l * 8 + 8],
                     self.norm2_w[l].rearrange("(kc p) -> p kc", p=128), cds, W=W, slow=True)
            self.dma("sp", self.PRM[:, self.P_ADAB + l * 48:self.P_ADAB + l * 48 + 48],
                     self.ada_b[l].rearrange("(c p) -> p c", p=128), cds, W=W, slow=True)
            self.dma("sp", self.PRM[:, self.P_PSC + l * 4:self.P_PSC + l * 4 + 4],
                     self.pool_scale[l].rearrange("(c p) -> p c", p=128), cds, W=W, slow=True)
            for h in range(2):
                self.dma("sp", self.PRM[64 * h:64 * h + 64, self.P_QNW + l:self.P_QNW + l + 1],
                         self.q_norm_w[l].rearrange("(p o) -> p o", o=1), cds, W=W, slow=True)
                self.dma("sp", self.PRM[64 * h:64 * h + 64, self.P_KNW + l:self.P_KNW + l + 1],
                         self.k_norm_w[l].rearrange("(p o) -> p o", o=1), cds, W=W, slow=True)
            self.dma("pool", self.PW[:, l], self.pool_w[l].rearrange("g c d -> c g d"), cds, W=[self.CBb])
        t = self.attn_sink.tensor
        for k in range(2):
            src = bass.AP(t, 4 * k, [[0, 64], [8, 2], [1, 4]])
            self.dma("sp", self.PRM[64 * k:64 * k + 64, self.P_SINK:self.P_SINK + 8].rearrange("p (l h) -> p l h", l=2),
                     src, cds, W=W, slow=True)
        self.dma("sp", self.PRM[:, self.P_LGB:self.P_LGB + 16], self.ret_decay.partition_broadcast(128), cds, W=W, slow=True)
        t = self.ret_decay.tensor
        for s in range(2):
            src = bass.AP(t, s, [[0, 64], [2, 8]])
            self.dma("sp", self.PRM[64 * s:64 * s + 64, self.P_LGC:self.P_LGC + 8], src, cds, W=W, slow=True)
        self.S.add("pool", lambda e: e.memset(self.ZHALO[:], 0.0), (), [self.ZHALOb])
        self.act(self.prm(self.P_SINK, 8), self.prm(self.P_SINK, 8), AF.Exp, R=[self.PRMb], W=[self.PRMb])
        for c0, n in ((self.P_LGB, 16), (self.P_LGC, 8)):
            self.act(self.prm(c0, n), self.prm(c0, n), AF.Exp, scale=-float(np.log(2.0)), R=[self.PRMb], W=[self.PRMb])
            self.act(self.prm(c0, n), self.prm(c0, n), AF.Ln, scale=-1.0, bias=1.0, R=[self.PRMb], W=[self.PRMb])
        self.act(self.prm(self.P_GC, 8), self.prm(self.P_LGC, 8), AF.Exp, scale=128.0, R=[self.PRMb], W=[self.PRMb])

    def layer_tables(self, l):
        if self.cur_l == l:
            return
        self.cur_l = l
        R = [self.PRMb, self.CFb]
        W = [self.LTb]
        if True:
            for h in range(4):
                lf = self.prm(self.P_LGB + l * 8 + h)
                lb = self.prm(self.P_LGB + l * 8 + 4 + h)
                f1, f1b = self.f32r.next()
                self.act(f1[:, 0:128], self.CF[:, CF_RELP:CF_RELP + 128], AF.Exp, scale=lf, R=R, W=[f1b])
                f2, f2b = self.f32r.next()
                self.act(f2[:, 0:128], self.CF[:, CF_RELN:CF_RELN + 128], AF.Exp, scale=lb, R=R, W=[f2b])
                self.tt("dve", self.DT[:, h, :], f1[:, 0:128], f2[:, 0:128], ALU.add, R=[f1b, f2b], W=W)
            for d in range(2):
                colc = CF_COL127 if d == 0 else CF_COLJ
                self.act(self.ZETA[:, d, :], self.prm(self.P_LGB + l * 8 + d * 4, 4), AF.Exp,
                         scale=self.CF[:, colc:colc + 1], R=R, W=W)
                for c in range(2):
                    io = CF_IOTA1 if d == 0 else CF_IOTAR
                    self.act(self.XI[:, d, c, :], self.CF[:, io:io + 128], AF.Exp,
                             scale=self.prm(self.P_LGC + l * 4 + d * 2 + c), R=R, W=W)

    SL_A0, SL_RV, SL_B0, SL_RG, SL_AQ, SL_PU = 0, 1, 2, 3, 4, 5
    SL_G = 6
    SL_BO = 14
    SL_WO = 22
    SL_W1 = 24
    SL_W2 = 32

    def slab_view(self, l, sid, kc, ncols):
        return self.wscr[l, sid][:, 0:kc * ncols].rearrange("p (kc c) -> p kc c", kc=kc)

    def prep_weights(self):
        ds = self.prepds
        for l in range(DEPTH):
            W = [self.wscrb[l]]
            win = self.w_in[l]

            def wcols(c0, n):
                return win[:, c0:c0 + n].rearrange("(kc p) c -> p kc c", p=128)

            def put(sid, off, n, src, kc=KC, tot=512):
                dst = self.slab_view(l, sid, kc, tot)[:, :, off:off + n]
                self.dma("pool", dst, src, ds, W=W)
            put(self.SL_A0, 0, 256, wcols(C_RK, 256))
            put(self.SL_A0, 256, 256, wcols(C_AK, 256))
            put(self.SL_RV, 0, 512, wcols(C_RV, 512))
            put(self.SL_B0, 0, 256, wcols(C_RQ, 256))
            put(self.SL_B0, 256, 256, wcols(C_RK, 256))
            put(self.SL_RG, 0, 512, wcols(C_RG, 512))
            put(self.SL_AQ, 0, 512, wcols(C_AQ, 512))
            put(self.SL_PU, 0, 512, wcols(C_PU, 512))
            for x in range(3):
                for dm in range(8):
                    put(self.SL_G + dm, x * 128, 128, wcols(C_GT + x * 1024 + dm * 128, 128), tot=384)
            for x, wt in enumerate((self.w_ret_out, self.w_pool_out, self.w_attn_out)):
                for dm in range(8):
                    dst = self.wscr[l, self.SL_BO + dm][:, x * 512:(x + 1) * 512].rearrange("p (kc c) -> p kc c", kc=4)
                    if x < 2:
                        src = wt[l][:, dm * 128:(dm + 1) * 128].rearrange("(kc p) c -> p kc c", p=128)
                        self.dma("pool", dst, src, ds, W=W)
                    else:
                        for k in range(2):
                            dsth = self.wscr[l, self.SL_BO + dm][64 * k:64 * k + 64, x * 512:(x + 1) * 512].rearrange("p (kc c) -> p kc c", kc=4)
                            src = wt[l][256 * k:256 * k + 256, dm * 128:(dm + 1) * 128].rearrange("(kc p) c -> p kc c", p=64)
                            self.dma("pool", dsth, src, ds, W=W)
            for q in range(2):
                put(self.SL_WO + q, 0, 512, self.w_out[l][:, q * 512:(q + 1) * 512].rearrange("(kc p) c -> p kc c", p=128))
            for i in range(8):
                put(self.SL_W1 + i, 0, 512, self.w_mlp1[l][:, i * 512:(i + 1) * 512].rearrange("(kc p) c -> p kc c", p=128))
            for hq in range(4):
                for q in range(2):
                    put(self.SL_W2 + hq * 2 + q, 0, 512,
                        self.w_mlp2[l][hq * 1024:(hq + 1) * 1024, q * 512:(q + 1) * 512].rearrange("(kc p) c -> p kc c", p=128))

    def load_slab(self, l, sid, n):
        t, b, ds = self.slabr.next()
        self.dma("sp", t[:, 0:n], self.wscr[l, sid][:, 0:n], ds, R=[self.wscrb[l]], W=[b])
        return t, b

    def mod_vectors(self):
        f, fb = self.f32r.next()
        self.dma("sp", f[:, 0:24].rearrange("p (kc n) -> p kc n", n=3), self.cT.rearrange("(kc p) n -> p kc n", p=128),
                 self.cds, W=[fb], slow=True)
        self.act(self.SCT[:].rearrange("p kc n -> p (kc n)"), f[:, 0:24], AF.Silu, R=[fb], W=[self.MODb])
        for l in range(DEPTH):
            bt, bb = self.bank()
            for sidx in range(12):
                t, b, ds = self.slabr.next()
                self.dma("pool", t[:].rearrange("p (kc c) -> p kc c", kc=KC),
                         self.ada_w[l][:, sidx * 512:(sidx + 1) * 512].rearrange("(kc p) c -> p kc c", p=128), ds, W=[b])
                tv = t[:].rearrange("p (kc c) -> p kc c", kc=KC)
                for c in range(4):
                    ch = sidx * 4 + c
                    for kc in range(KC):
                        self.mm(bt[:, ch * 3:ch * 3 + 3], tv[:, kc, c * 128:(c + 1) * 128], self.SCT[:, kc, :],
                                start=(kc == 0), stop=(kc == KC - 1), R=[b, self.MODb], W=[bb])
            adab = self.PRM[:, self.P_ADAB + l * 48:self.P_ADAB + l * 48 + 48]
            self.tt("dve", self.MODT[:, l], bt[:, 0:144].rearrange("p (c n) -> p c n", n=3), bc_last(adab, 3), ALU.add,
                    R=[bb, self.PRMb], W=[self.MODb])
            for who in range(3):
                for j, (sc0, nw) in enumerate(((8, self.P_N1W), (32, self.P_N2W))):
                    self.stt("dve", self.MODA[:, l, who, j * 8:(j + 1) * 8], self.MODT[:, l, sc0:sc0 + 8, who], 1.0,
                             self.PRM[:, nw + l * 8:nw + l * 8 + 8], ALU.add, ALU.mult, R=[self.MODb, self.PRMb], W=[self.MODb])

    def modcol(self, l, who, ch):
        return self.MODT[:, l, ch, who:who + 1]

    def run_batch(self, b):
        self.dma("sp", self.XC[:], self.cxT[b].rearrange("(kc p) t -> p kc t", p=128), self.xcds, W=self.XCb)
        for ti in range(S // TL):
            self.dma("sp", self.XT[:, :, ti * TL:(ti + 1) * TL],
                     self.xT[b][:, ti * TL:(ti + 1) * TL].rearrange("(kc p) t -> p kc t", p=128), self.xds[ti], W=[self.XTb[ti]])
        ctx = dict(name="ctx", L=LC, T=LC, nt=1, X=self.XC, Xb=self.XCb, who=2, rope=False, HALO=self.HALOc, HALOb=self.HALOcb)
        lat = dict(name="lat", L=S, T=TL, nt=S // TL, X=self.XT, Xb=self.XTb, who=b, rope=True, HALO=self.HALO, HALOb=self.HALOb,
                   RF=lambda n: self.RF[:, n], RB=lambda n: self.RB[:, n], RFb=self.RFb, RBb=self.RBb,
                   AKT=self.AKT, AKTb=self.AKTb, AV=lambda n: self.AV[:, n], AVb=self.AVb)
        st = self.stage
        for l in range(DEPTH):
            c = dict(ctx)
            c.update(RF=lambda n: self.RFc[:, n], RB=lambda n: self.RBc[:, n], RFb=self.RFcb, RBb=self.RBcb,
                     AKT=self.AKTc[:, l], AKTb=[self.AKTcb[l]], AV=lambda n, l=l: self.AVc[:, l, n], AVb=[self.AVcb[l]])
            if l == 0 or st >= 5:
                self.phaseA(c, l, None)
            if l == 0 and st >= 4:
                self.phaseB(c, l, c, last=False)
        for l in range(DEPTH):
            c = dict(ctx)
            c.update(AKT=self.AKTc[:, l], AKTb=[self.AKTcb[l]], AV=lambda n, l=l: self.AVc[:, l, n], AVb=[self.AVcb[l]])
            if st >= 6 + 2 * l:
                self.phaseA(lat, l, l)
            if st >= 7 + 2 * l:
                self.phaseB(lat, l, c, last=(l == DEPTH - 1), b=b)
        if st < 99 and b == 0:
            self.dump("XC", self.XC[:].rearrange("p a b -> p (a b)"), self.XCb, [128, KC * LC], F32)
            self.dump("AKTc", self.AKTc[:].rearrange("p a b -> p (a b)"), self.AKTcb, [128, DEPTH * LC], BF16)
            self.dump("AVc", self.AVc[:].rearrange("p a b c -> p (a b c)"), self.AVcb, [128, DEPTH * 256], BF16)
            self.dump("R0", self.R0[:].rearrange("p a b c d -> p (a b c d)"), self.R0b[0] + self.R0b[1], [128, DEPTH * 512], F32)
            self.dump("RFc", self.RFc[:].rearrange("p a b c -> p (a b c)"), self.RFcb, [128, 512], BF16)
            self.dump("RBc", self.RBc[:].rearrange("p a b c -> p (a b c)"), self.RBcb, [128, 512], BF16)
            self.dump("UX", self.UX[:].rearrange("p a b -> p (a b)"), [self.UXb], [128, KC * TL], BF16)
            self.dump("KT", self.KT, self.KTb, [128, 2, TL], BF16)
            self.dump("VT", self.VT, self.VTb, [128, 4, 512], BF16)
            self.dump("YRT", self.YRT[:], self.YRTb, [128, 4, TL], BF16)
            self.dump("YAT", self.YAT[:], self.YATb, [128, 4, TL], BF16)
            self.dump("YPT", self.YPT[:], self.YPTb, [128, 4, TL], BF16)
            self.dump("XT", self.XT[:].rearrange("p a b -> p (a b)"), self.XTb, [128, KC * S], F32)
            self.dump("RB", self.RB[:].rearrange("p a b c -> p (a b c)"), self.RBb, [128, 4096], BF16)
            self.dump("AKT", self.AKT[:], self.AKTb, [128, S], BF16)

    def norm_tile(self, sq, l, ti, which):
        T, X, who = sq["T"], sq["X"], sq["who"]
        xb = sq["Xb"][ti]
        t0 = ti * T
        xs = X[:, :, t0:t0 + T]
        self.act(self.UX[:, :, 0:T], xs, AF.Square, R=[xb], W=[self.UXb])
        bt, bb = self.bank()
        for kc in range(KC):
            self.mm(bt[:, 0:T], self.CB[:, CB_ONES1024:CB_ONES1024 + 128], self.UX[:, kc, 0:T],
                    start=(kc == 0), stop=(kc == KC - 1), R=[self.UXb, self.CBb], W=[bb])
        self.rsqrt(self.RSTD[:, 0:T], bt[:, 0:T], R=[bb], W=[self.RSTDb])
        shc = 0 if which == 0 else 24
        for kc in range(KC):
            f, fb = self.f32r.next()
            self.stt("dve", f[:, 0:T], X[:, kc, t0:t0 + T], self.MODA[:, l, who, which * 8 + kc:which * 8 + kc + 1],
                     self.RSTD[:, 0:T], ALU.mult, ALU.mult, R=[xb, self.RSTDb, self.MODb], W=[fb])
            self.act(self.UX[:, kc, 0:T], f[:, 0:T], AF.Identity, bias=self.modcol(l, who, shc + kc), R=[fb, self.MODb], W=[self.UXb])

    def proj_fm(self, slab, sb_, kcn, tot, c0, T, m=128, rhs=None, rhsb=None, N=None):
        bt, bb = self.bank()
        sv = slab[:, 0:kcn * tot].rearrange("p (kc c) -> p kc c", kc=kcn)
        if rhs is None:
            rhs = lambda kc: self.UX[:, kc, 0:T]
            rhsb = [self.UXb]
        N = N or T
        for kc in range(kcn):
            self.mm(bt[0:m, 0:N], sv[:, kc, c0:c0 + m], rhs(kc), start=(kc == 0), stop=(kc == kcn - 1), R=[sb_] + rhsb, W=[bb])
        return bt, bb

    def load_tab(self, which, t0, T):
        self.dma("sp", self.TAB[:, :, 0:T], self.rope_d[2 * which:2 * which + 2, :, t0:t0 + T].rearrange("a p t -> p a t"),
                 self.TABds, W=[self.TABb])

    def rope_from(self, src, srcb, pm_col, out, outb, T, extraR=()):
        bt, bb = self.bank()
        self.mm(bt[:, 0:T], self.CB[:, pm_col:pm_col + 128], src, R=[srcb, self.CBb], W=[bb])
        f1, f1b = self.f32r.next()
        self.tt("dve", f1[:, 0:T], src, self.TAB[:, 0, 0:T], ALU.mult, R=[srcb, self.TABb], W=[f1b])
        f2, f2b = self.f32r.next()
        self.tt("dve", f2[:, 0:T], bt[:, 0:T], self.TAB[:, 1, 0:T], ALU.mult, R=[bb, self.TABb], W=[f2b])
        self.tt("pool", out, f1[:, 0:T], f2[:, 0:T], ALU.add, R=[f1b, f2b], W=[outb])

    def ret_k_chunk(self, sq, slab, sb_, c0, c, T):
        bt, bb = self.proj_fm(slab, sb_, KC, 512, c0 + c * 128, T)
        if sq["rope"]:
            x, xb = self.b16r.next()
            self.act(x[:, 0:T], bt[:, 0:T], AF.Copy, R=[bb], W=[xb])
            self.rope_from(x[:, 0:T], xb, CB_PMR, self.KT[:, c, 0:T], self.KTb[c], T)
        else:
            self.act(self.KT[:, c, 0:T], bt[:, 0:T], AF.Copy, R=[bb], W=[self.KTb[c]])

    def v_chunks(self, slab, sb_, T, c0=0, tot=512):
        sv = slab[:, 0:KC * tot].rearrange("p (kc c) -> p kc c", kc=KC)
        for j in range(T // 128):
            bt, bb = self.bank()
            for kc in range(KC):
                self.mm(bt[:, 0:512], self.UX[:, kc, j * 128:(j + 1) * 128], sv[:, kc, c0:c0 + 512],
                        start=(kc == 0), stop=(kc == KC - 1), R=[sb_, self.UXb], W=[bb])
            self.cp("dve" if j % 2 == 0 else "pool" if False else "dve", self.VT[:, j, :], bt[:, 0:512], R=[bb], W=[self.VTb[j]])

    def phaseA(self, sq, l, r0l):
        CUT = CUT0 if (not KSEQ or KSEQ == sq["name"]) else 99.0
        self.layer_tables(l)
        T, nt = sq["T"], sq["nt"]
        nch = T // 128
        ntot = nt * nch
        for d in range(2):
            if r0l is None:
                self.S.add("pool", lambda e, d=d: e.memset(self.ST[:, d, 0], 0.0), (), [self.STb[d][0]])
            else:
                self.cp("pool", self.ST[:, d, 0], self.R0[:, r0l, d], R=[self.R0b[r0l][d]], W=[self.STb[d][0]])
        pp = 0
        for ti in range(nt):
            t0 = ti * T
            self.norm_tile(sq, l, ti, 0)
            if CUT <= 1:
                return
            hb = sq["HALOb"][ti]
            self.cp("pool", sq["HALO"][:, ti, 0], self.UX[:, :, 0:8], R=[self.UXb], W=[hb])
            self.cp("pool", sq["HALO"][:, ti, 1], self.UX[:, :, T - 8:T], R=[self.UXb], W=[hb])
            slab, sb_ = self.load_slab(l, self.SL_A0, 4096)
            if sq["rope"]:
                self.load_tab(0, t0, T)
            if CUT <= 2:
                return
            for c in range(2):
                self.ret_k_chunk(sq, slab, sb_, 0, c, T)
            if CUT <= 3:
                return
            bt, bb = self.proj_fm(slab, sb_, KC, 512, 256, T)
            zc, zcb = self.f32r.next()
            self.cp("dve", zc[:, 0:T], bt[:, 0:T], R=[bb], W=[zcb])
            x, xb = self.b16r.next()
            self.act(x[:, 0:T], zc[:, 0:T], AF.Square, R=[zcb], W=[xb])
            b2, b2b = self.bank()
            self.mm(b2[:, 0:T], self.CB[:, CB_BD64:CB_BD64 + 128], x[:, 0:T], R=[xb, self.CBb], W=[b2b])
            rs, rsb = self.f32r.next()
            self.rsqrt(rs[:, 0:T], b2[:, 0:T], R=[b2b], W=[rsb])
            akb = sq["AKTb"][ti]
            if sq["rope"]:
                zn, znb = self.b16r.next()
                self.stt("dve", zn[:, 0:T], zc[:, 0:T], self.prm(self.P_KNW + l), rs[:, 0:T], ALU.mult, ALU.mult, R=[zcb, rsb, self.PRMb], W=[znb])
                if CUT <= 3.2:
                    return
                self.load_tab(1, t0, T)
                if CUT <= 3.3:
                    return
                self.rope_from(zn[:, 0:T], znb, CB_PMA, sq["AKT"][:, t0:t0 + T], akb, T)
            else:
                self.stt("dve", sq["AKT"][:, t0:t0 + T], zc[:, 0:T], self.prm(self.P_KNW + l), rs[:, 0:T], ALU.mult, ALU.mult, R=[zcb, rsb, self.PRMb], W=[akb])
            if CUT <= 4:
                return
            sv = slab[:, 0:4096].rearrange("p (kc c) -> p kc c", kc=KC)
            for j in range(nch):
                b3, b3b = self.bank()
                for kc in range(KC):
                    self.mm(b3[:, 0:128], self.UX[:, kc, j * 128:(j + 1) * 128], sv[:, kc, 384:512],
                            start=(kc == 0), stop=(kc == KC - 1), R=[sb_, self.UXb], W=[b3b])
                self.act(sq["AV"](ti * nch + j), b3[:, 0:128], AF.Copy, R=[b3b], W=[sq["AVb"][ti]])
            if CUT <= 5:
                return
            slab2, sb2 = self.load_slab(l, self.SL_RV, 4096)
            self.v_chunks(slab2, sb2, T)
            if CUT <= 6:
                return
            for j in range(nch):
                n = ti * nch + j
                kz, kzb = self.b16r.next()
                for c in range(2):
                    bt, bb = self.bank()
                    btb = bt.bitcast(BF16) if hasattr(bt, "bitcast") else bt
                    pt = btb[:, 0:128]
                    self.S.add("pe", lambda e, o=pt, i=self.KT[:, c, j * 128:(j + 1) * 128], idn=self.CB[:, CB_IDENT:CB_IDENT + 128]:
                               e.transpose(o, i, idn), [self.KTb[c], self.CBb], [bb])
                    for s in range(2):
                        h = 2 * c + s
                        for d in range(2):
                            dst = kz[:, d * 256 + h * 64:d * 256 + h * 64 + 64]
                            if c == 0:
                                self.act(dst, pt[:, s * 64:(s + 1) * 64], AF.Copy, scale=self.ZETA[:, d, h:h + 1], R=[bb, self.LTb], W=[kzb])
                            else:
                                self.ts("dve", dst, pt[:, s * 64:(s + 1) * 64], self.ZETA[:, d, h:h + 1], None, ALU.mult, R=[bb, self.LTb], W=[kzb])
                if CUT <= 7:
                    return
                ub, ubb = self.bank()
                for d in range(2):
                    for h in range(4):
                        c, s = h // 2, h % 2
                        self.mm(ub[64 * s:64 * s + 64, (d * 2 + c) * 128:(d * 2 + c + 1) * 128],
                                kz[:, d * 256 + h * 64:d * 256 + h * 64 + 64], self.VT[:, j, h * 128:(h + 1) * 128],
                                tp=(0, 64 * s), R=[kzb, self.VTb[j]], W=[ubb])
                if CUT <= 8:
                    return
                self.cp("pool", sq["RF"](n), self.ST[:, 0, pp], R=[self.STb[0][pp]], W=[sq["RFb"][n]])
                if CUT <= 8.1:
                    return
                for c in range(2):
                    self.stt("dve", self.ST[:, 0, 1 - pp, c], self.ST[:, 0, pp, c], self.prm(self.P_GC + l * 4 + c),
                             ub[:, c * 128:(c + 1) * 128], ALU.mult, ALU.add, R=[self.STb[0][pp], ubb, self.PRMb], W=[self.STb[0][1 - pp]])
                if CUT <= 8.2:
                    return
                self.cp("dve", sq["RB"](n), ub[:, 256:512].rearrange("p (c e) -> p c e", c=2), R=[ubb], W=[sq["RBb"][n]])
                pp = 1 - pp
                if CUT <= 8.3:
                    return
        ppf = pp
        if CUT <= 9:
            return
        pb = 0
        for n in range(ntot - 1, -1, -1):
            f, fb = self.f32r.next()
            self.cp("pool", f[:, 0:256].rearrange("p (c e) -> p c e", c=2), sq["RB"](n), R=[sq["RBb"][n]], W=[fb])
            self.cp("pool", sq["RB"](n), self.ST[:, 1, pb], R=[self.STb[1][pb], fb], W=[sq["RBb"][n]])
            for c in range(2):
                self.stt("dve", self.ST[:, 1, 1 - pb, c], self.ST[:, 1, pb, c], self.prm(self.P_GC + l * 4 + 2 + c),
                         f[:, c * 128:(c + 1) * 128], ALU.mult, ALU.add, R=[self.STb[1][pb], fb, self.PRMb], W=[self.STb[1][1 - pb]])
            pb = 1 - pb
        if r0l is None:
            self.cp("pool", self.R0[:, l, 0], self.ST[:, 0, ppf], R=[self.STb[0][ppf]], W=[self.R0b[l][0]])
            self.cp("pool", self.R0[:, l, 1], self.ST[:, 1, pb], R=[self.STb[1][pb]], W=[self.R0b[l][1]])

    def phaseB(self, sq, l, cx, last, b=0):
        T, nt, who = sq["T"], sq["nt"], sq["who"]
        nch = T // 128
        CUT = CUT0 if (not KSEQ or KSEQ == sq["name"]) else 99.0
        self.layer_tables(l)
        for ti in range(nt):
            t0 = ti * T
            xb = sq["Xb"][ti]
            X = sq["X"]
            self.norm_tile(sq, l, ti, 0)
            slab, sb_ = self.load_slab(l, self.SL_B0, 4096)
            if sq["rope"]:
                self.load_tab(0, t0, T)
            for c in range(2):
                bt, bb = self.proj_fm(slab, sb_, KC, 512, c * 128, T)
                if sq["rope"]:
                    x, xbf = self.b16r.next()
                    self.act(x[:, 0:T], bt[:, 0:T], AF.Copy, scale=0.125, R=[bb], W=[xbf])
                    self.rope_from(x[:, 0:T], xbf, CB_PMR, self.QT[:, c, 0:T], self.QTb[c], T)
                else:
                    self.act(self.QT[:, c, 0:T], bt[:, 0:T], AF.Copy, scale=0.125, R=[bb], W=[self.QTb[c]])
                qv = self.QT[:, c, 0:T].rearrange("p (j i) -> p j i", i=128)
                self.tt("pool", self.QXF[:, c, 0:T].rearrange("p (j i) -> p j i", i=128), qv, bc_mid(self.XI[:, 0, c, :], nch), ALU.mult,
                        R=[self.QTb[c], self.LTb], W=[self.QXFb[c]])
                self.tt("pool", self.QXB[:, c, 0:T].rearrange("p (j i) -> p j i", i=128), qv, bc_mid(self.XI[:, 1, c, :], nch), ALU.mult,
                        R=[self.QTb[c], self.LTb], W=[self.QXBb[c]])
            if CUT <= 11:
                return
            for c in range(2):
                self.ret_k_chunk(sq, slab, sb_, 256, c, T)
            slab, sb_ = self.load_slab(l, self.SL_RV, 4096)
            self.v_chunks(slab, sb_, T)
            slab, sb_ = self.load_slab(l, self.SL_RG, 4096)
            for h in range(4):
                bt, bb = self.proj_fm(slab, sb_, KC, 512, h * 128, T)
                self.act(self.SRG[:, h, 0:T], bt[:, 0:T], AF.Silu, R=[bb], W=[self.SRGb[h]])
            if CUT <= 12:
                return
            for j in range(nch):
                n = ti * nch + j
                js = slice(j * 128, (j + 1) * 128)
                sbk = [self.bank(), self.bank()]
                for h in range(4):
                    c, s = h // 2, h % 2
                    self.mm(sbk[s][0][:, c * 128:(c + 1) * 128], self.KT[64 * s:64 * s + 64, c, js], self.QT[64 * s:64 * s + 64, c, js],
                            tp=(64 * s, 0), R=[self.KTb[c], self.QTb[c]], W=[sbk[s][1]])
                p, pb_ = self.b16r.next()
                dtv = self.DT[:].rearrange("p (c s) i -> p c s i", s=2)
                pv = p[:, 0:512].rearrange("p (c s i) -> p c s i", c=2, s=2)
                for s in range(2):
                    self.tt("dve", pv[:, :, s, :], sbk[s][0][:, 0:256].rearrange("p (c i) -> p c i", c=2), dtv[:, :, s, :], ALU.mult,
                            R=[sbk[s][1], self.LTb], W=[pb_])
                if CUT <= 12.1:
                    return
                yb, ybb = self.bank()
                for h in range(4):
                    c, s = h // 2, h % 2
                    o = yb[:, h * 128:(h + 1) * 128]
                    self.mm(o, self.VT[:, j, h * 128:(h + 1) * 128], p[:, h * 128:(h + 1) * 128], start=True, stop=False,
                            R=[self.VTb[j], pb_], W=[ybb])
                    self.mm(o, sq["RF"](n)[64 * s:64 * s + 64, c, :], self.QXF[64 * s:64 * s + 64, c, js], start=False, stop=False,
                            tp=(64 * s, 0), R=[sq["RFb"][n], self.QXFb[c]], W=[ybb])
                    self.mm(o, sq["RB"](n)[64 * s:64 * s + 64, c, :], self.QXB[64 * s:64 * s + 64, c, js], start=False, stop=True,
                            tp=(64 * s, 0), R=[sq["RBb"][n], self.QXBb[c]], W=[ybb])
                if CUT <= 12.2:
                    return
                yraw, yrawb = self.f32r.next()
                self.cp("dve", yraw[:, 0:512], yb[:, 0:512], R=[ybb], W=[yrawb])
                x, xbf = self.b16r.next()
                self.act(x[:, 0:512], yraw[:, 0:512], AF.Square, R=[yrawb], W=[xbf])
                mb, mbb = self.bank()
                self.mm(mb[:, 0:512], self.CB[:, CB_ONES128:CB_ONES128 + 128], x[:, 0:512], R=[xbf, self.CBb], W=[mbb])
                rs, rsb = self.f32r.next()
                self.rsqrt(rs[:, 0:512], mb[:, 0:512], R=[mbb], W=[rsb])
                if CUT <= 12.3:
                    return
                yn, ynb = self.f32r.next()
                self.tt("dve", yn[:, 0:512], yraw[:, 0:512], rs[:, 0:512], ALU.mult, R=[yrawb, rsb], W=[ynb])
                if CUT <= 12.4:
                    return
                self.tt("pool", self.YRT[:, :, js], yn[:, 0:512].rearrange("p (h i) -> p h i", h=4), self.SRG[:, :, js], ALU.mult,
                        R=[ynb] + self.SRGb, W=self.YRTb)
            if CUT <= 13:
                return
            slab, sb_ = self.load_slab(l, self.SL_AQ, 4096)
            for cq in range(4):
                bt, bb = self.proj_fm(slab, sb_, KC, 512, cq * 128, T)
                zc, zcb = self.f32r.next()
                self.cp("dve", zc[:, 0:T], bt[:, 0:T], R=[bb], W=[zcb])
                x, xbf = self.b16r.next()
                self.act(x[:, 0:T], zc[:, 0:T], AF.Square, R=[zcb], W=[xbf])
                b2, b2b = self.bank()
                self.mm(b2[:, 0:T], self.CB[:, CB_BD64:CB_BD64 + 128], x[:, 0:T], R=[xbf, self.CBb], W=[b2b])
                rs, rsb = self.f32r.next()
                self.rsqrt(rs[:, 0:T], b2[:, 0:T], R=[b2b], W=[rsb])
                self.stt("dve", self.ZN[:, cq, 0:T], zc[:, 0:T], self.prm(self.P_QNW + l), rs[:, 0:T], ALU.mult, ALU.mult,
                         R=[zcb, rsb, self.PRMb], W=[self.ZNb[cq]])
            if sq["rope"]:
                self.load_tab(1, t0, T)
            for cp_ in range(4):
                s = cp_ % 2
                srcA, srcB = cp_ // 2, 2 + cp_ // 2
                bt, bb = self.bank()
                self.mm(bt[:, 0:T], self.CB[:, CB_SELA0 + s * 128:CB_SELA0 + (s + 1) * 128], self.ZN[:, srcA, 0:T], start=True, stop=False,
                        R=[self.ZNb[srcA], self.CBb], W=[bb])
                self.mm(bt[:, 0:T], self.CB[:, CB_SELB0 + s * 128:CB_SELB0 + (s + 1) * 128], self.ZN[:, srcB, 0:T], start=False, stop=True,
                        R=[self.ZNb[srcB], self.CBb], W=[bb])
                if sq["rope"]:
                    b2, b2b = self.bank()
                    self.mm(b2[:, 0:T], self.CB[:, CB_SWA0 + s * 128:CB_SWA0 + (s + 1) * 128], self.ZN[:, srcA, 0:T], start=True, stop=False,
                            R=[self.ZNb[srcA], self.CBb], W=[b2b])
                    self.mm(b2[:, 0:T], self.CB[:, CB_SWB0 + s * 128:CB_SWB0 + (s + 1) * 128], self.ZN[:, srcB, 0:T], start=False, stop=True,
                            R=[self.ZNb[srcB], self.CBb], W=[b2b])
                    f1, f1b = self.f32r.next()
                    self.tt("dve", f1[:, 0:T], bt[:, 0:T], self.TAB[:, 0, 0:T], ALU.mult, R=[bb, self.TABb], W=[f1b])
                    f2, f2b = self.f32r.next()
                    self.tt("dve", f2[:, 0:T], b2[:, 0:T], self.TAB[:, 1, 0:T], ALU.mult, R=[b2b, self.TABb], W=[f2b])
                    self.tt("pool", self.AQT[:, cp_, 0:T], f1[:, 0:T], f2[:, 0:T], ALU.add, R=[f1b, f2b], W=[self.AQTb[cp_]])
                else:
                    self.act(self.AQT[:, cp_, 0:T], bt[:, 0:T], AF.Copy, R=[bb], W=[self.AQTb[cp_]])
            if CUT <= 14:
                return
            nblk = sq["L"] // 128
            for j in range(nch):
                n = ti * nch + j
                js = slice(j * 128, (j + 1) * 128)
                blocks = []
                if sq["name"] == "lat":
                    for dn, mk in ((-1, CB_MASKP), (0, None), (1, CB_MASKN)):
                        nn = n + dn
                        if 0 <= nn < nblk:
                            blocks.append((sq["AKT"][:, nn * 128:(nn + 1) * 128], sq["AV"](nn),
                                           [sq["AKTb"][nn // nch], sq["AVb"][nn // nch]], mk))
                for cn in range(LC // 128):
                    blocks.append((cx["AKT"][:, cn * 128:(cn + 1) * 128], cx["AV"](cn), [cx["AKTb"][0], cx["AVb"][0]], None))
                obe = self.banks.next()
                dbe = self.banks.next()
                (ob, obb), (db, dbb) = obe, dbe
                lring = Ring([e for e in self.banks.items if e is not obe and e is not dbe])
                for k in range(2):
                    for bi, (kap, vap, kvb, mk) in enumerate(blocks):
                        lb, lbb = lring.next()
                        self.mm(lb[:, 0:512], kap[64 * k:64 * k + 64, :], self.AQT[64 * k:64 * k + 64, :, js], start=True, stop=(mk is None),
                                tp=(64 * k, 0), R=kvb + self.AQTb, W=[lbb])
                        if mk is not None:
                            self.mm(lb[:, 0:512], self.CB[:, mk:mk + 128], self.CB[:, CB_IDENT2:CB_IDENT2 + 512], start=False, stop=True,
                                    R=[self.CBb], W=[lbb])
                        pt, ptb = self.ptr.next()
                        self.act(pt[:, 0:512], lb[:, 0:512], AF.Exp, scale=0.125, R=[lbb], W=[ptb])
                        first, lastb = (bi == 0), (bi == len(blocks) - 1)
                        self.mm(ob[64 * k:64 * k + 64, 0:512], vap[:, 64 * k:64 * k + 64], pt[:, 0:512], start=first, stop=lastb,
                                tp=(0, 64 * k), R=kvb + [ptb], W=[obb])
                        self.mm(db[64 * k:64 * k + 64, 0:512], self.CB[:, CB_ONES:CB_ONES + 64], pt[:, 0:512], start=first, stop=lastb,
                                tp=(0, 64 * k), R=[ptb, self.CBb], W=[dbb])
                dn_, dnb = self.f32r.next()
                self.tt("dve", dn_[:, 0:512].rearrange("p (h i) -> p h i", h=4), db[:, 0:512].rearrange("p (h i) -> p h i", h=4),
                        bc_last(self.PRM[:, self.P_SINK + l * 4:self.P_SINK + l * 4 + 4], 128), ALU.add, R=[dbb, self.PRMb], W=[dnb])
                rd, rdb = self.f32r.next()
                self.recip(rd[:, 0:512], dn_[:, 0:512], R=[dnb], W=[rdb])
                self.tt("dve", self.YAT[:, :, js], ob[:, 0:512].rearrange("p (h i) -> p h i", h=4),
                        rd[:, 0:512].rearrange("p (h i) -> p h i", h=4), ALU.mult, R=[obb, rdb], W=self.YATb)
            if CUT <= 15:
                return
            slab, sb_ = self.load_slab(l, self.SL_PU, 4096)
            wins = (2, 4, 8, 16)
            hl = (sq["HALO"][:, ti - 1, 1], sq["HALOb"][ti - 1]) if ti > 0 else (self.ZHALO[:], self.ZHALOb)
            hr = (sq["HALO"][:, ti + 1, 0], sq["HALOb"][ti + 1]) if ti < nt - 1 else (self.ZHALO[:], self.ZHALOb)
            for g in range(4):
                w = wins[g]
                bt, bb = self.proj_fm(slab, sb_, KC, 512, g * 128, T)
                pu, pub = self.f32r.next()
                self.act(pu[:, 8:8 + T], bt[:, 0:T], AF.Copy, R=[bb], W=[pub])
                b2, b2b = self.bank()
                sv = slab[:, 0:4096].rearrange("p (kc c) -> p kc c", kc=KC)
                for side, (hap, hbuf) in enumerate((hl, hr)):
                    for kc in range(KC):
                        self.mm(b2[:, side * 8:side * 8 + 8], sv[:, kc, g * 128:(g + 1) * 128], hap[:, kc, :],
                                start=(kc == 0), stop=(kc == KC - 1), R=[sb_, hbuf], W=[b2b])
                self.cp("dve", pu[:, 0:8], b2[:, 0:8], R=[b2b], W=[pub])
                self.cp("dve", pu[:, 8 + T:16 + T], b2[:, 8:16], R=[b2b], W=[pub])
                cur, curb, span = pu, pub, 1
                lo = 8 - w // 2
                width = T + w
                while span < w // 2:
                    nx, nxb = self.f32r.next()
                    wd = T + w - 2 * span + 1 - 1
                    self.tt("pool", nx[:, lo:lo + wd], cur[:, lo:lo + wd], cur[:, lo + span:lo + span + wd], ALU.add, R=[curb], W=[nxb])
                    cur, curb, span = nx, nxb, span * 2
                sm, smb = self.f32r.next()
                self.tt("pool", sm[:, 0:T], cur[:, lo:lo + T], cur[:, lo + w // 2:lo + w // 2 + T], ALU.add, R=[curb], W=[smb])
                if ti == 0:
                    self.tt("pool", sm[:, 0:8], sm[:, 0:8], self.CF[:, CF_CORRL + g * 8:CF_CORRL + g * 8 + 8], ALU.mult, R=[smb, self.CFb], W=[smb])
                if ti == nt - 1:
                    self.tt("pool", sm[:, T - 8:T], sm[:, T - 8:T], self.CF[:, CF_CORRR + g * 8:CF_CORRR + g * 8 + 8], ALU.mult, R=[smb, self.CFb], W=[smb])
                df, dfb = self.b16r.next()
                self.stt("dve", df[:, 0:T], sm[:, 0:T], 1.0 / w, pu[:, 8:8 + T], ALU.mult, ALU.subtract, R=[smb, pub], W=[dfb])
                b3, b3b = self.bank()
                self.mm(b3[:, 0:T], self.PW[:, l, g, :], df[:, 0:T], R=[dfb, self.CBb], W=[b3b])
                self.act(self.YPT[:, g, 0:T], b3[:, 0:T], AF.Copy, scale=self.prm(self.P_PSC + l * 4 + g), R=[b3b, self.PRMb], W=[self.YPTb[g]])
            if CUT <= 16:
                return
            ysrc = ((self.YRT, self.YRTb), (self.YPT, self.YPTb), (self.YAT, self.YATb))
            for dm in range(8):
                gs, gsb = self.load_slab(l, self.SL_G + dm, KC * 384)
                bs, bsb = self.load_slab(l, self.SL_BO + dm, 1536)
                acc, accb = self.f32r.next()
                for x in range(3):
                    gb, gbb = self.proj_fm(gs, gsb, KC, 384, x * 128, T)
                    sg, sgb = self.f32r.next()
                    self.act(sg[:, 0:T], gb[:, 0:T], AF.Sigmoid, R=[gbb], W=[sgb])
                    yt_, ytb_ = ysrc[x]
                    pb2, pbb = self.proj_fm(bs[:, x * 512:(x + 1) * 512], bsb, 4, 128, 0, T,
                                            rhs=lambda kc, yt_=yt_: yt_[:, kc, 0:T], rhsb=list(ytb_))
                    if x == 0:
                        self.tt("dve", acc[:, 0:T], sg[:, 0:T], pb2[:, 0:T], ALU.mult, R=[sgb, pbb], W=[accb])
                    else:
                        self.tt("dve", sg[:, 0:T], sg[:, 0:T], pb2[:, 0:T], ALU.mult, R=[sgb, pbb], W=[sgb])
                        if x == 1:
                            self.tt("pool", acc[:, 0:T], acc[:, 0:T], sg[:, 0:T], ALU.add, R=[sgb, accb], W=[accb])
                        else:
                            self.tt("pool", self.YT[:, dm, 0:T], acc[:, 0:T], sg[:, 0:T], ALU.add, R=[sgb, accb], W=[self.YTb[dm]])
            if CUT <= 17:
                return
            for q in range(2):
                slab, sb_ = self.load_slab(l, self.SL_WO + q, 4096)
                for dd in range(4):
                    dm = q * 4 + dd
                    bt, bb = self.proj_fm(slab, sb_, KC, 512, dd * 128, T, rhs=lambda kc: self.YT[:, kc, 0:T], rhsb=list(self.YTb))
                    self.stt("dve", X[:, dm, t0:t0 + T], bt[:, 0:T], self.modcol(l, who, 16 + dm), X[:, dm, t0:t0 + T], ALU.mult, ALU.add,
                             R=[bb, self.MODb, xb], W=[xb])
            if CUT <= 18:
                return
            self.norm_tile(sq, l, ti, 1)
            for hq in range(4):
                for i2 in range(2):
                    slab, sb_ = self.load_slab(l, self.SL_W1 + hq * 2 + i2, 4096)
                    for cc in range(4):
                        hc = i2 * 4 + cc
                        bt, bb = self.proj_fm(slab, sb_, KC, 512, cc * 128, T)
                        r, rb_ = self.f32r.next()
                        self.act(r[:, 0:T], bt[:, 0:T], AF.Relu, R=[bb], W=[rb_])
                        self.tt("pool", self.HT[:, hc, 0:T], r[:, 0:T], r[:, 0:T], ALU.mult, R=[rb_], W=[self.HTb[hc]])
                for q in range(2):
                    slab, sb_ = self.load_slab(l, self.SL_W2 + hq * 2 + q, 4096)
                    for dd in range(4):
                        dm = q * 4 + dd
                        bt, bb = self.proj_fm(slab, sb_, KC, 512, dd * 128, T, rhs=lambda kc: self.HT[:, kc, 0:T], rhsb=list(self.HTb))
                        self.stt("dve", X[:, dm, t0:t0 + T], bt[:, 0:T], self.modcol(l, who, 40 + dm), X[:, dm, t0:t0 + T], ALU.mult, ALU.add,
                                 R=[bb, self.MODb, xb], W=[xb])
            if last:
                self.dma("sp", self.yT[b][:, t0:t0 + T].rearrange("(kc p) t -> p kc t", p=128), X[:, :, t0:t0 + T], self.outds,
                         R=[xb], W=[self.outb])


_PROG = None
_CONSTS = None


def kernel(**inputs):
    global _PROG, _CONSTS
    if _CONSTS is None:
        _CONSTS = _host_consts()
    cf, cb, rope = _CONSTS
    if _PROG is None:
        _PROG = Prog()
    prog = _PROG
    x = np.asarray(inputs["x"], np.float32)
    ctx = np.asarray(inputs["ctx"], np.float32)
    c = np.asarray(inputs["c"], np.float32)
    c_ctx = np.asarray(inputs["c_ctx"], np.float32)
    shared = {}
    for k in ("norm1_w", "norm2_w", "ada_w", "ada_b", "w_in", "pool_w", "pool_scale", "q_norm_w", "k_norm_w",
              "w_ret_out", "w_pool_out", "w_attn_out", "w_out", "w_mlp1", "w_mlp2"):
        shared[k] = np.ascontiguousarray(np.asarray(inputs[k], np.float32))
    shared["ret_decay"] = np.ascontiguousarray(np.asarray(inputs["ret_decay"], np.float32).reshape(-1))
    shared["attn_sink"] = np.ascontiguousarray(np.asarray(inputs["attn_sink"], np.float32).reshape(-1))
    shared["cstf"], shared["cstb"], shared["rope"] = cf, cb, rope
    in_maps = []
    for core in range(NCORES):
        bs = slice(core * NB, (core + 1) * NB)
        m = dict(shared)
        m["xT"] = np.ascontiguousarray(x[bs].transpose(0, 2, 1))
        m["cxT"] = np.ascontiguousarray(ctx[bs].transpose(0, 2, 1))
        m["cT"] = np.ascontiguousarray(np.stack([c[core * NB], c[core * NB + 1], c_ctx], axis=1))
        in_maps.append(m)
    res = run_bass_kernel_spmd(prog.nc, in_maps, core_ids=list(range(NCORES)))
    out = np.empty((NCORES * NB, S, D), np.float32)
    for core in range(NCORES):
        out[core * NB:(core + 1) * NB] = res.results[core]["yT"].transpose(0, 2, 1)
    return out
```

```python
import os
import numpy as np
from contextlib import ExitStack
CUT0 = float(os.environ.get("KCUT", "99"))
KSEQ = os.environ.get("KSEQ", "")
CUT = 99.0
import concourse.bass as bass
import concourse.mybir as mybir
from concourse.bass_utils import run_bass_kernel_spmd

F32 = mybir.dt.float32
BF16 = mybir.dt.bfloat16
AF = mybir.ActivationFunctionType
ALU = mybir.AluOpType

NCORES = 8
NB = 2
D = 1024
KC = 8
S = 2048
LC = 256
DEPTH = 2
TL = 512
EPS = 1e-6
NEG = -30000.0
SAME_ENGINE_SYNC = True

C_RK, C_RV, C_AK, C_AV = 0, 256, 768, 896
C_RQ, C_RG, C_AQ, C_PU, C_GT = 1024, 1280, 1792, 2304, 2816

CF_RELP, CF_RELN, CF_IOTA1, CF_IOTAR, CF_COLJ, CF_COL127, CF_CORRL, CF_CORRR, CF_W = 0, 128, 256, 384, 512, 513, 514, 546, 584
(CB_ONES1024, CB_ONES128, CB_BD64, CB_IDENT, CB_PMR, CB_PMA, CB_SELA0, CB_SELA1, CB_SELB0, CB_SELB1,
 CB_SWA0, CB_SWA1, CB_SWB0, CB_SWB1, CB_MASKP, CB_MASKN) = [i * 128 for i in range(16)]
CB_IDENT2 = 16 * 128
CB_ONES = CB_IDENT2 + 512
CB_W = CB_ONES + 64


def _host_consts():
    j = np.arange(128, dtype=np.float64)[:, None]
    i = np.arange(128, dtype=np.float64)[None, :]
    cf = np.zeros((128, CF_W), np.float32)
    BIG = 1.0e7
    cf[:, CF_RELP:CF_RELP + 128] = np.where(i >= j, i - j, BIG)
    cf[:, CF_RELN:CF_RELN + 128] = np.where(j >= i, j - i, BIG)
    cf[:, CF_IOTA1:CF_IOTA1 + 128] = i + 1
    cf[:, CF_IOTAR:CF_IOTAR + 128] = 128 - i
    cf[:, CF_COLJ] = j[:, 0]
    cf[:, CF_COL127] = 127 - j[:, 0]
    wins = (2, 4, 8, 16)
    for g, w in enumerate(wins):
        for t in range(8):
            cntl = min(t + w // 2, 10 ** 9) - max(t - w // 2, 0)
            cf[:, CF_CORRL + g * 8 + t] = w / cntl
            tt = 7 - t
            hi = min((10 ** 6 - 1 - tt) + w // 2, 10 ** 6)
            lo = (10 ** 6 - 1 - tt) - w // 2
            cf[:, CF_CORRR + g * 8 + t] = w / (hi - lo)
    cb = np.zeros((128, CB_W), np.float32)
    cb[:, CB_ONES1024:CB_ONES1024 + 128] = 1.0 / 1024
    cb[:, CB_ONES128:CB_ONES128 + 128] = 1.0 / 128
    k = np.arange(128)[:, None]
    m = np.arange(128)[None, :]
    cb[:, CB_BD64:CB_BD64 + 128] = (k // 64 == m // 64) / 64.0
    cb[:, CB_IDENT:CB_IDENT + 128] = (k == m)
    swr = np.where(m % 64 < 32, m + 32, m - 32)
    cb[:, CB_PMR:CB_PMR + 128] = (k == swr)
    swa = np.where(m % 32 < 16, m + 16, m - 16)
    cb[:, CB_PMA:CB_PMA + 128] = (k == swa)
    for s in range(2):
        selA = (m < 64) & (k == 64 * s + m)
        selB = (m >= 64) & (k == 64 * s + (m - 64))
        mm = m % 64
        swm = np.where(mm % 32 < 16, mm + 16, mm - 16)
        swA = (m < 64) & (k == 64 * s + swm)
        swB = (m >= 64) & (k == 64 * s + swm)
        cb[:, CB_SELA0 + s * 128:CB_SELA0 + (s + 1) * 128] = selA
        cb[:, CB_SELB0 + s * 128:CB_SELB0 + (s + 1) * 128] = selB
        cb[:, CB_SWA0 + s * 128:CB_SWA0 + (s + 1) * 128] = swA
        cb[:, CB_SWB0 + s * 128:CB_SWB0 + (s + 1) * 128] = swB
    cb[:, CB_MASKP:CB_MASKP + 128] = np.where(m < k, NEG, 0.0)
    cb[:, CB_MASKN:CB_MASKN + 128] = np.where(m > k, NEG, 0.0)
    cb[:, CB_IDENT2:CB_IDENT2 + 512] = np.tile((k == m).astype(np.float32), (1, 4))
    cb[:, CB_ONES:CB_ONES + 64] = 1.0
    rope = np.zeros((4, 128, S), np.float32)
    p = np.arange(128)
    d = p % 64
    t = np.arange(S)
    fr = (np.float32(10000.0) ** (-np.arange(32, dtype=np.float32) / np.float32(32))).astype(np.float32)
    ang = (t.astype(np.float32)[None, :] * fr[d % 32][:, None]).astype(np.float32).astype(np.float64)
    rope[0] = np.cos(ang)
    rope[1] = np.sin(ang) * np.where(d < 32, -1.0, 1.0)[:, None]
    fa = (np.float32(10000.0) ** (-np.arange(16, dtype=np.float32) / np.float32(16))).astype(np.float32)
    row = (t // 64).astype(np.float32)
    col = (t % 64).astype(np.float32)
    pos = np.where((d < 32)[:, None], row[None, :], col[None, :]).astype(np.float32)
    anga = (pos * fa[d % 16][:, None]).astype(np.float32).astype(np.float64)
    rope[2] = np.cos(anga)
    rope[3] = np.sin(anga) * np.where(d % 32 < 16, -1.0, 1.0)[:, None]
    return cf, cb, rope


class DSem:
    def __init__(self, sem):
        self.sem = sem
        self.total = 0


class Buf:
    __slots__ = ("name", "w", "re", "rd", "const", "alias")

    def __init__(self, name, const=False):
        self.name = name
        self.w = None
        self.re = {}
        self.rd = {}
        self.const = const
        self.alias = []


class Op:
    __slots__ = ("eng", "fn", "edeps", "ddeps", "inc", "cnt", "dsem", "dval", "idx")


class Sched:
    ENG = ["pe", "act", "dve", "pool", "sp"]

    def __init__(self, nc, stack):
        self.nc = nc
        self.ops = {e: [] for e in self.ENG}
        self.esem = {e: stack.enter_context(nc.semaphore("es_" + e)) for e in self.ENG[:4]}
        self.stack = stack
        self.n = 0

    def dsem(self, name):
        return DSem(self.stack.enter_context(self.nc.semaphore(name)))

    @staticmethod
    def _adddep(ed, dd, dep):
        if dep is None:
            return
        if dep[0] == "e":
            o = dep[1]
            cur = ed.get(o.eng)
            if cur is None or cur.idx < o.idx:
                ed[o.eng] = o
        else:
            _, ds, v = dep
            if dd.get(ds, 0) < v:
                dd[ds] = v

    def add(self, eng, fn, reads=(), writes=(), dsem=None):
        o = Op()
        o.eng, o.fn, o.dsem, o.inc, o.cnt = eng, fn, dsem, False, None
        o.idx = self.n
        self.n += 1
        ed, dd = {}, {}
        for b in reads:
            self._adddep(ed, dd, b.w)
            for a in b.alias:
                self._adddep(ed, dd, a.w)
        for b in writes:
            for bb in [b] + b.alias:
                self._adddep(ed, dd, bb.w)
                for r in bb.re.values():
                    self._adddep(ed, dd, ("e", r))
                for ds, v in bb.rd.items():
                    self._adddep(ed, dd, ("d", ds, v))
        o.edeps, o.ddeps = ed, dd
        if dsem is not None:
            dsem.total += 16
            o.dval = dsem.total
            me = ("d", dsem, o.dval)
        else:
            me = ("e", o)
        for b in reads:
            if b.const:
                continue
            if dsem is not None:
                b.rd[dsem] = o.dval
            else:
                b.re[eng] = o
        for b in writes:
            b.w = me
            b.re = {}
            b.rd = {}
        self.ops[eng].append(o)
        return o

    def finalize(self, block):
        for e in self.ENG:
            for o in self.ops[e]:
                for pe_, p in list(o.edeps.items()):
                    if p.dsem is not None:
                        continue
                    if p.eng == o.eng:
                        if o.eng == "pe" or not SAME_ENGINE_SYNC or o.dsem is not None and False:
                            del o.edeps[pe_]
                            continue
                    p.inc = True
        for e in self.ENG[:4]:
            c = 0
            for o in self.ops[e]:
                if o.dsem is None and o.inc:
                    c += 1
                    o.cnt = c
        nc = self.nc

        def emit(engname, engobj):
            waited = {}
            for o in self.ops[engname]:
                for p in o.edeps.values():
                    sem = self.esem[p.eng]
                    if waited.get(sem.name, 0) < p.cnt:
                        engobj.wait_ge(sem, p.cnt)
                        waited[sem.name] = p.cnt
                for ds, v in o.ddeps.items():
                    if waited.get(ds.sem.name, 0) < v:
                        engobj.wait_ge(ds.sem, v)
                        waited[ds.sem.name] = v
                if o.fn is None:
                    continue
                ins = o.fn(engobj)
                if o.dsem is not None:
                    ins.then_inc(o.dsem.sem, 16)
                elif o.inc:
                    ins.then_inc(self.esem[engname], 1)

        block.tensor(lambda e: emit("pe", e))
        block.scalar(lambda e: emit("act", e))
        block.vector(lambda e: emit("dve", e))
        block.gpsimd(lambda e: emit("pool", e))
        block.sync(lambda e: emit("sp", e))


class Ring:
    def __init__(self, items):
        self.items = items
        self.i = 0

    def next(self):
        it = self.items[self.i % len(self.items)]
        self.i += 1
        return it


def bc_mid(ap, n):
    pairs = [list(x) for x in ap.ap]
    return bass.AP(ap.tensor, ap.offset, [pairs[0], [0, n]] + pairs[1:])


def bc_last(ap, n):
    pairs = [list(x) for x in ap.ap]
    return bass.AP(ap.tensor, ap.offset, pairs + [[0, n]])


class Prog:
    def __init__(self, dbg=None, stage=99):
        self.dbg = dbg or {}
        self.stage = stage
        self.dumps = []
        self.stack = ExitStack()
        self.nc = bass.Bass("TRN2", target_bir_lowering=False)
        self.build()

    def sb(self, name, shape, dt):
        return self.stack.enter_context(self.nc.sbuf_tensor(name, list(shape), dt))

    def din(self, name, shape, dt=F32):
        return self.nc.dram_tensor(name, list(shape), dt, kind="ExternalInput").ap()

    def mm(self, out, lhsT, rhs, start=True, stop=True, tp=None, R=(), W=()):
        kw = dict(start=start, stop=stop)
        if tp is not None:
            kw["tile_position"] = tp
        self.S.add("pe", lambda e, o=out, l=lhsT, r=rhs, kw=kw: e.matmul(o, l, r, **kw), R, W)

    def act(self, out, in_, func, bias=None, scale=None, R=(), W=()):
        kw = {}
        if bias is not None:
            kw["bias"] = bias
        if scale is not None:
            kw["scale"] = scale
        self.S.add("act", lambda e, o=out, i=in_, f=func, kw=kw: e.activation(out=o, in_=i, func=f, **kw), R, W)

    def tt(self, eng, out, in0, in1, op, R=(), W=()):
        self.S.add(eng, lambda e, o=out, a=in0, b=in1, op=op: e.tensor_tensor(out=o, in0=a, in1=b, op=op), R, W)

    def ts(self, eng, out, in0, s1, s2, op0, op1=None, R=(), W=()):
        if op1 is None:
            self.S.add(eng, lambda e, o=out, a=in0, s1=s1, op0=op0: e.tensor_scalar(o, a, s1, None, op0), R, W)
        else:
            self.S.add(eng, lambda e, o=out, a=in0, s1=s1, s2=s2, op0=op0, op1=op1: e.tensor_scalar(o, a, s1, s2, op0, op1), R, W)

    def stt(self, eng, out, in0, scalar, in1, op0, op1, R=(), W=()):
        self.S.add(eng, lambda e, o=out, a=in0, s=scalar, b=in1, op0=op0, op1=op1:
                   e.scalar_tensor_tensor(out=o, in0=a, scalar=s, in1=b, op0=op0, op1=op1), R, W)

    def cp(self, eng, out, in_, R=(), W=()):
        self.S.add(eng, lambda e, o=out, i=in_: e.tensor_copy(out=o, in_=i), R, W)

    def rsqrt(self, out, in_, R=(), W=()):
        self.act(out, in_, AF.Sqrt, bias=EPS, R=R, W=W)
        self.recip(out, out, R=W, W=W)

    def recip(self, out, in_, R=(), W=()):
        self.S.add("dve", lambda e, o=out, i=in_: e.reciprocal(out=o, in_=i), R, W)

    def dma(self, q, out, in_, dsem, R=(), W=(), slow=False):
        kw = {"allow_slow_non_contiguous": True} if slow else {}
        self.S.add(q, lambda e, o=out, i=in_, kw=kw: e.dma_start(out=o, in_=i, **kw), R, W, dsem=dsem)

    def dump(self, name, ap, bufs, shape, dt):
        d = self.nc.dram_tensor("dbg_" + name, list(shape), dt, kind="ExternalOutput").ap()
        self.dma("sp", d, ap, self.outds, R=list(bufs), W=[self.outb])
        self.dumps.append(name)

    def bank(self):
        return self.banks.next()

    def build(self):
        nc, st = self.nc, self.stack
        din = self.din
        self.xT = din("xT", [NB, D, S])
        self.cxT = din("cxT", [NB, D, LC])
        self.cT = din("cT", [D, 3])
        self.norm1_w = din("norm1_w", [DEPTH, D])
        self.norm2_w = din("norm2_w", [DEPTH, D])
        self.ada_w = din("ada_w", [DEPTH, D, 6 * D])
        self.ada_b = din("ada_b", [DEPTH, 6 * D])
        self.w_in = din("w_in", [DEPTH, D, 5888])
        self.ret_decay = din("ret_decay", [DEPTH * 2 * 4])
        self.pool_w = din("pool_w", [DEPTH, 4, 128, 128])
        self.pool_scale = din("pool_scale", [DEPTH, 512])
        self.q_norm_w = din("q_norm_w", [DEPTH, 64])
        self.k_norm_w = din("k_norm_w", [DEPTH, 64])
        self.attn_sink = din("attn_sink", [DEPTH * 8])
        self.w_ret_out = din("w_ret_out", [DEPTH, 512, D])
        self.w_pool_out = din("w_pool_out", [DEPTH, 512, D])
        self.w_attn_out = din("w_attn_out", [DEPTH, 512, D])
        self.w_out = din("w_out", [DEPTH, D, D])
        self.w_mlp1 = din("w_mlp1", [DEPTH, D, 4 * D])
        self.w_mlp2 = din("w_mlp2", [DEPTH, 4 * D, D])
        self.cstf_d = din("cstf", [128, CF_W])
        self.cstb_d = din("cstb", [128, CB_W])
        self.rope_d = din("rope", [4, 128, S])
        self.yT = nc.dram_tensor("yT", [NB, D, S], F32, kind="ExternalOutput").ap()
        self.NSLAB = 2 + 4 + 16 + 2 + 8 + 16
        self.wscr = nc.dram_tensor("wscr", [DEPTH, self.NSLAB, 128, 4096], BF16).ap()

        self.S = Sched(nc, st)
        S_ = self.S
        self.banks = Ring([])
        for i in range(8):
            t = st.enter_context(nc.psum_tensor("bank%d" % i, [128, 512], F32))
            self.banks.items.append((t, Buf("bank%d" % i)))
        sb = self.sb
        self.XT = sb("XT", [128, KC, S], F32)
        self.XTb = [Buf("XT%d" % i) for i in range(S // TL)]
        self.UX = sb("UX", [128, KC, TL], BF16)
        self.UXb = Buf("UX")
        self.RSTD = sb("RSTD", [128, TL], F32)
        self.RSTDb = Buf("RSTD")
        self.RFX = sb("RFX", [128, KC * LC], F32)
        self.RF = self.RFX.bitcast(BF16)[:].rearrange("p (n c e) -> p n c e", n=16, c=2)
        self.XC = self.RFX[:].rearrange("p (kc t) -> p kc t", kc=KC)
        self.RB = sb("RB", [128, 16, 2, 128], BF16)
        self.RFb = [Buf("RF%d" % i) for i in range(16)]
        self.RBb = [Buf("RB%d" % i) for i in range(16)]
        self.AKT = sb("AKT", [128, S], BF16)
        self.AKTb = [Buf("AKT%d" % i) for i in range(S // TL)]
        self.AV = sb("AV", [128, 16, 128], BF16)
        self.AVb = [Buf("AV%d" % i) for i in range(S // TL)]
        self.HALO = sb("HALO", [128, 4, 2, KC, 8], BF16)
        self.HALOb = [Buf("HALO%d" % i) for i in range(4)]
        self.ZHALO = sb("ZHALO", [128, KC, 8], BF16)
        self.ZHALOb = Buf("ZHALO", const=True)
        self.XCb = [Buf("XC")]
        self.XCb[0].alias = list(self.RFb)
        for b_ in self.RFb:
            b_.alias = list(self.XCb)
        self.RFc = sb("RFc", [128, 2, 2, 128], BF16)
        self.RBc = sb("RBc", [128, 2, 2, 128], BF16)
        self.RFcb = [Buf("RFc%d" % i) for i in range(2)]
        self.RBcb = [Buf("RBc%d" % i) for i in range(2)]
        self.AKTc = sb("AKTc", [128, DEPTH, LC], BF16)
        self.AKTcb = [Buf("AKTc%d" % l) for l in range(DEPTH)]
        self.AVc = sb("AVc", [128, DEPTH, 2, 128], BF16)
        self.AVcb = [Buf("AVc%d" % l) for l in range(DEPTH)]
        self.HALOc = sb("HALOc", [128, 1, 2, KC, 8], BF16)
        self.HALOcb = [Buf("HALOc")]
        self.R0 = sb("R0", [128, DEPTH, 2, 2, 128], F32)
        self.R0b = [[Buf("R0%d%d" % (l, d)) for d in range(2)] for l in range(DEPTH)]
        self.ST = sb("ST", [128, 2, 2, 2, 128], F32)
        self.STb = [[Buf("ST%d%d" % (d, i)) for i in range(2)] for d in range(2)]
        self.R1 = sb("R1", [128, 4096], BF16)
        self.KT = self.R1[:, 0:1024].rearrange("p (c t) -> p c t", c=2)
        self.VT = self.R1[:, 1024:3072].rearrange("p (c t) -> p c t", c=4)
        self.QT = self.R1[:, 3072:4096].rearrange("p (c t) -> p c t", c=2)
        self.HT = self.R1[:].rearrange("p (c t) -> p c t", c=8)
        self.QXF = sb("QXF", [128, 2, TL], BF16)
        self.QXB = sb("QXB", [128, 2, TL], BF16)
        self.AQT = sb("AQT", [128, 4, TL], BF16)
        self.KTb = [Buf("KT0"), Buf("KT1")]
        self.VTb = [Buf("VT%d" % i) for i in range(4)]
        self.QTb = [Buf("QT0"), Buf("QT1")]
        self.QXFb = [Buf("QXF0"), Buf("QXF1")]
        self.QXBb = [Buf("QXB0"), Buf("QXB1")]
        self.AQTb = [Buf("AQT%d" % i) for i in range(4)]
        self.HTb = [Buf("HT%d" % i) for i in range(8)]
        r1 = self.KTb + self.VTb + self.QTb
        for b_ in self.HTb:
            b_.alias = list(r1)
        for b_ in r1:
            b_.alias = list(self.HTb)
        self.YRT = sb("YRT", [128, 4, TL], BF16)
        self.YAT = sb("YAT", [128, 4, TL], BF16)
        self.YPT = sb("YPT", [128, 4, TL], BF16)
        self.YRTb = [Buf("YRT%d" % i) for i in range(4)]
        self.YATb = [Buf("YAT%d" % i) for i in range(4)]
        self.YPTb = [Buf("YPT%d" % i) for i in range(4)]
        self.R2 = sb("R2", [128, 4096], BF16)
        self.ZN = self.R2[:, 2048:4096].rearrange("p (c t) -> p c t", c=4)
        self.ZNb = [Buf("ZN%d" % i) for i in range(4)]
        self.SRG = self.R2[:, 0:2048].rearrange("p (c t) -> p c t", c=4)
        self.SRGb = [Buf("SRG%d" % i) for i in range(4)]
        self.YT = self.R2[:].rearrange("p (c t) -> p c t", c=8)
        self.YTb = [Buf("YT%d" % i) for i in range(KC)]
        r2 = self.SRGb + self.ZNb
        for b_ in self.YTb:
            b_.alias = list(r2)
        for b_ in r2:
            b_.alias = list(self.YTb)
        self.f32r = Ring([(sb("F32R%d" % i, [128, 528], F32), Buf("F32R%d" % i)) for i in range(6)])
        self.b16r = Ring([(sb("B16R%d" % i, [128, 512], BF16), Buf("B16R%d" % i)) for i in range(5)])
        self.ptr = Ring([(sb("PTR%d" % i, [128, 512], BF16), Buf("PTR%d" % i)) for i in range(5)])
        self.slabr = Ring([(sb("SLAB%d" % i, [128, 4096], BF16), Buf("SLAB%d" % i), S_.dsem("ds_slab%d" % i)) for i in range(2)])
        self.TAB = sb("TAB", [128, 2, TL], F32)
        self.TABb = Buf("TAB")
        self.TABds = S_.dsem("ds_tab")
        self.CF = sb("CF", [128, CF_W], F32)
        self.CB = sb("CB", [128, CB_W], BF16)
        self.CFb = Buf("CF", const=True)
        self.CBb = Buf("CB", const=True)
        self.PW = sb("PW", [128, DEPTH, 4, 128], BF16)
        self.PRM = sb("PRM", [128, 256], F32)
        self.PRMb = Buf("PRM", const=True)
        self.MODT = sb("MODT", [128, DEPTH, 48, 3], F32)
        self.MODA = sb("MODA", [128, DEPTH, 3, 16], F32)
        self.MODb = Buf("MOD", const=True)
        self.SCT = sb("SCT", [128, KC, 3], BF16)
        self.DT = sb("DT", [128, 4, 128], F32)
        self.XI = sb("XI", [128, 2, 2, 128], F32)
        self.ZETA = sb("ZETA", [128, 2, 4], F32)
        self.LTb = Buf("LT")
        self.cur_l = None
        self.cds = S_.dsem("ds_const")
        self.prepds = S_.dsem("ds_prep")
        self.xds = [S_.dsem("ds_x%d" % i) for i in range(S // TL)]
        self.xcds = S_.dsem("ds_xc")
        self.outds = S_.dsem("ds_out")
        self.wscrb = [Buf("wscr%d" % l) for l in range(DEPTH)]
        self.outb = Buf("out")
        self.dbgds = S_.dsem("ds_dbg")

        self.P_N1W, self.P_N2W = 0, 16
        self.P_ADAB = 32
        self.P_PSC = 128
        self.P_QNW, self.P_KNW = 136, 138
        self.P_SINK = 140
        self.P_LGB = 148
        self.P_LGC = 164
        self.P_GC = 172
        self.P_TMP = 180

        self.setup_consts()
        if self.stage >= 1:
            self.prep_weights([0])
            self.prep1_done = False
        if self.stage >= 2:
            self.mod_vectors()
        if self.stage >= 3:
            for b in range(NB if self.stage >= 99 else 1):
                self.run_batch(b)
        if self.stage < 99:
            self.dump("PRM", self.PRM[:], [self.PRMb], [128, 256], F32)
            self.dump("MODT", self.MODT[:].rearrange("p a b c -> p (a b c)"), [self.MODb], [128, DEPTH * 48 * 3], F32)
        self.S.add("sp", None, reads=[self.outb])
        with nc.Block() as block:
            self.S.finalize(block)

    def prm(self, c0, n=1):
        return self.PRM[:, c0:c0 + n]

    def setup_consts(self):
        nc = self.nc
        cds = self.cds
        W = [self.PRMb]
        self.dma("sp", self.CF[:], self.cstf_d[:, :], cds, W=[self.CFb])
        self.dma("pool", self.CB[:], self.cstb_d[:, :], cds, W=[self.CBb])
        for l in range(DEPTH):
            self.dma("sp", self.PRM[:, self.P_N1W + l * 8:self.P_N1W + l * 8 + 8],
                     self.norm1_w[l].rearrange("(kc p) -> p kc", p=128), cds, W=W, slow=True)
            self.dma("sp", self.PRM[:, self.P_N2W + l * 8:self.P_N2W + l * 8 + 8],
                     self.norm2_w[l].rearrange("(kc p) -> p kc", p=128), cds, W=W, slow=True)
            self.dma("sp", self.PRM[:, self.P_ADAB + l * 48:self.P_ADAB + l * 48 + 48],
                     self.ada_b[l].rearrange("(c p) -> p c", p=128), cds, W=W, slow=True)
            self.dma("sp", self.PRM[:, self.P_PSC + l * 4:self.P_PSC + l * 4 + 4],
                     self.pool_scale[l].rearrange("(c p) -> p c", p=128), cds, W=W, slow=True)
            for h in range(2):
                self.dma("sp", self.PRM[64 * h:64 * h + 64, self.P_QNW + l:self.P_QNW + l + 1],
                         self.q_norm_w[l].rearrange("(p o) -> p o", o=1), cds, W=W, slow=True)
                self.dma("sp", self.PRM[64 * h:64 * h + 64, self.P_KNW + l:self.P_KNW + l + 1],
                         self.k_norm_w[l].rearrange("(p o) -> p o", o=1), cds, W=W, slow=True)
            self.dma("pool", self.PW[:, l], self.pool_w[l].rearrange("g c d -> c g d"), cds, W=[self.CBb])
        t = self.attn_sink.tensor
        for k in range(2):
            src = bass.AP(t, 4 * k, [[0, 64], [8, 2], [1, 4]])
            self.dma("sp", self.PRM[64 * k:64 * k + 64, self.P_SINK:self.P_SINK + 8].rearrange("p (l h) -> p l h", l=2),
                     src, cds, W=W, slow=True)
        self.dma("sp", self.PRM[:, self.P_LGB:self.P_LGB + 16], self.ret_decay.partition_broadcast(128), cds, W=W, slow=True)
        t = self.ret_decay.tensor
        for s in range(2):
            src = bass.AP(t, s, [[0, 64], [2, 8]])
            self.dma("sp", self.PRM[64 * s:64 * s + 64, self.P_LGC:self.P_LGC + 8], src, cds, W=W, slow=True)
        self.S.add("pool", lambda e: e.memset(self.ZHALO[:], 0.0), (), [self.ZHALOb])
        self.act(self.prm(self.P_SINK, 8), self.prm(self.P_SINK, 8), AF.Exp, R=[self.PRMb], W=[self.PRMb])
        for c0, n in ((self.P_LGB, 16), (self.P_LGC, 8)):
            self.act(self.prm(c0, n), self.prm(c0, n), AF.Exp, scale=-float(np.log(2.0)), R=[self.PRMb], W=[self.PRMb])
            self.act(self.prm(c0, n), self.prm(c0, n), AF.Ln, scale=-1.0, bias=1.0, R=[self.PRMb], W=[self.PRMb])
        self.act(self.prm(self.P_GC, 8), self.prm(self.P_LGC, 8), AF.Exp, scale=128.0, R=[self.PRMb], W=[self.PRMb])

    def layer_tables(self, l):
        if self.cur_l == l:
            return
        self.cur_l = l
        R = [self.PRMb, self.CFb]
        W = [self.LTb]
        if True:
            for h in range(4):
                lf = self.prm(self.P_LGB + l * 8 + h)
                lb = self.prm(self.P_LGB + l * 8 + 4 + h)
                f1, f1b = self.f32r.next()
                self.act(f1[:, 0:128], self.CF[:, CF_RELP:CF_RELP + 128], AF.Exp, scale=lf, R=R, W=[f1b])
                f2, f2b = self.f32r.next()
                self.act(f2[:, 0:128], self.CF[:, CF_RELN:CF_RELN + 128], AF.Exp, scale=lb, R=R, W=[f2b])
                self.tt("dve", self.DT[:, h, :], f1[:, 0:128], f2[:, 0:128], ALU.add, R=[f1b, f2b], W=W)
            for d in range(2):
                colc = CF_COL127 if d == 0 else CF_COLJ
                self.act(self.ZETA[:, d, :], self.prm(self.P_LGB + l * 8 + d * 4, 4), AF.Exp,
                         scale=self.CF[:, colc:colc + 1], R=R, W=W)
                for c in range(2):
                    io = CF_IOTA1 if d == 0 else CF_IOTAR
                    self.act(self.XI[:, d, c, :], self.CF[:, io:io + 128], AF.Exp,
                             scale=self.prm(self.P_LGC + l * 4 + d * 2 + c), R=R, W=W)

    SL_A0, SL_RV, SL_B0, SL_RG, SL_AQ, SL_PU = 0, 1, 2, 3, 4, 5
    SL_G = 6
    SL_BO = 14
    SL_WO = 22
    SL_W1 = 24
    SL_W2 = 32

    def slab_view(self, l, sid, kc, ncols):
        return self.wscr[l, sid][:, 0:kc * ncols].rearrange("p (kc c) -> p kc c", kc=kc)

    def prep_weights(self, layers=None):
        ds = self.prepds
        for l in (range(DEPTH) if layers is None else layers):
            W = [self.wscrb[l]]
            win = self.w_in[l]

            def wcols(c0, n):
                return win[:, c0:c0 + n].rearrange("(kc p) c -> p kc c", p=128)

            def put(sid, off, n, src, kc=KC, tot=512):
                dst = self.slab_view(l, sid, kc, tot)[:, :, off:off + n]
                self.dma("pool", dst, src, ds, W=W)
            put(self.SL_A0, 0, 256, wcols(C_RK, 256))
            put(self.SL_A0, 256, 256, wcols(C_AK, 256))
            put(self.SL_RV, 0, 512, wcols(C_RV, 512))
            put(self.SL_B0, 0, 256, wcols(C_RQ, 256))
            put(self.SL_B0, 256, 256, wcols(C_RK, 256))
            put(self.SL_RG, 0, 512, wcols(C_RG, 512))
            put(self.SL_AQ, 0, 512, wcols(C_AQ, 512))
            put(self.SL_PU, 0, 512, wcols(C_PU, 512))
            for x in range(3):
                for dm in range(8):
                    put(self.SL_G + dm, x * 128, 128, wcols(C_GT + x * 1024 + dm * 128, 128), tot=384)
            for x, wt in enumerate((self.w_ret_out, self.w_pool_out, self.w_attn_out)):
                for dm in range(8):
                    dst = self.wscr[l, self.SL_BO + dm][:, x * 512:(x + 1) * 512].rearrange("p (kc c) -> p kc c", kc=4)
                    if x < 2:
                        src = wt[l][:, dm * 128:(dm + 1) * 128].rearrange("(kc p) c -> p kc c", p=128)
                        self.dma("pool", dst, src, ds, W=W)
                    else:
                        for k in range(2):
                            dsth = self.wscr[l, self.SL_BO + dm][64 * k:64 * k + 64, x * 512:(x + 1) * 512].rearrange("p (kc c) -> p kc c", kc=4)
                            src = wt[l][256 * k:256 * k + 256, dm * 128:(dm + 1) * 128].rearrange("(kc p) c -> p kc c", p=64)
                            self.dma("pool", dsth, src, ds, W=W)
            for q in range(2):
                put(self.SL_WO + q, 0, 512, self.w_out[l][:, q * 512:(q + 1) * 512].rearrange("(kc p) c -> p kc c", p=128))
            for i in range(8):
                put(self.SL_W1 + i, 0, 512, self.w_mlp1[l][:, i * 512:(i + 1) * 512].rearrange("(kc p) c -> p kc c", p=128))
            for hq in range(4):
                for q in range(2):
                    put(self.SL_W2 + hq * 2 + q, 0, 512,
                        self.w_mlp2[l][hq * 1024:(hq + 1) * 1024, q * 512:(q + 1) * 512].rearrange("(kc p) c -> p kc c", p=128))

    def load_slab(self, l, sid, n):
        t, b, ds = self.slabr.next()
        self.dma("sp", t[:, 0:n], self.wscr[l, sid][:, 0:n], ds, R=[self.wscrb[l]], W=[b])
        return t, b

    def mod_vectors(self):
        f, fb = self.f32r.next()
        self.dma("sp", f[:, 0:24].rearrange("p (kc n) -> p kc n", n=3), self.cT.rearrange("(kc p) n -> p kc n", p=128),
                 self.cds, W=[fb], slow=True)
        self.act(self.SCT[:].rearrange("p kc n -> p (kc n)"), f[:, 0:24], AF.Silu, R=[fb], W=[self.MODb])
        for l in range(DEPTH):
            bt, bb = self.bank()
            for sidx in range(12):
                t, b, ds = self.slabr.next()
                self.dma("pool", t[:].rearrange("p (kc c) -> p kc c", kc=KC),
                         self.ada_w[l][:, sidx * 512:(sidx + 1) * 512].rearrange("(kc p) c -> p kc c", p=128), ds, W=[b])
                tv = t[:].rearrange("p (kc c) -> p kc c", kc=KC)
                for c in range(4):
                    ch = sidx * 4 + c
                    for kc in range(KC):
                        self.mm(bt[:, ch * 3:ch * 3 + 3], tv[:, kc, c * 128:(c + 1) * 128], self.SCT[:, kc, :],
                                start=(kc == 0), stop=(kc == KC - 1), R=[b, self.MODb], W=[bb])
            adab = self.PRM[:, self.P_ADAB + l * 48:self.P_ADAB + l * 48 + 48]
            self.tt("dve", self.MODT[:, l], bt[:, 0:144].rearrange("p (c n) -> p c n", n=3), bc_last(adab, 3), ALU.add,
                    R=[bb, self.PRMb], W=[self.MODb])
            for who in range(3):
                for j, (sc0, nw) in enumerate(((8, self.P_N1W), (32, self.P_N2W))):
                    self.stt("dve", self.MODA[:, l, who, j * 8:(j + 1) * 8], self.MODT[:, l, sc0:sc0 + 8, who], 1.0,
                             self.PRM[:, nw + l * 8:nw + l * 8 + 8], ALU.add, ALU.mult, R=[self.MODb, self.PRMb], W=[self.MODb])

    def modcol(self, l, who, ch):
        return self.MODT[:, l, ch, who:who + 1]

    def run_batch(self, b):
        self.dma("sp", self.XC[:], self.cxT[b].rearrange("(kc p) t -> p kc t", p=128), self.xcds, W=self.XCb)
        for ti in range(S // TL):
            self.dma("sp", self.XT[:, :, ti * TL:(ti + 1) * TL],
                     self.xT[b][:, ti * TL:(ti + 1) * TL].rearrange("(kc p) t -> p kc t", p=128), self.xds[ti], W=[self.XTb[ti]])
        ctx = dict(name="ctx", L=LC, T=LC, nt=1, X=self.XC, Xb=self.XCb, who=2, rope=False, HALO=self.HALOc, HALOb=self.HALOcb)
        lat = dict(name="lat", L=S, T=TL, nt=S // TL, X=self.XT, Xb=self.XTb, who=b, rope=True, HALO=self.HALO, HALOb=self.HALOb,
                   RF=lambda n: self.RF[:, n], RB=lambda n: self.RB[:, n], RFb=self.RFb, RBb=self.RBb,
                   AKT=self.AKT, AKTb=self.AKTb, AV=lambda n: self.AV[:, n], AVb=self.AVb)
        st = self.stage
        for l in range(DEPTH):
            c = dict(ctx)
            c.update(RF=lambda n: self.RFc[:, n], RB=lambda n: self.RBc[:, n], RFb=self.RFcb, RBb=self.RBcb,
                     AKT=self.AKTc[:, l], AKTb=[self.AKTcb[l]], AV=lambda n, l=l: self.AVc[:, l, n], AVb=[self.AVcb[l]])
            if l == 0 or st >= 5:
                self.phaseA(c, l, None)
            if l == 0 and not self.prep1_done:
                self.prep_weights([1])
                self.prep1_done = True
            if l == 0 and st >= 4:
                self.phaseB(c, l, c, last=False)
        for l in range(DEPTH):
            c = dict(ctx)
            c.update(AKT=self.AKTc[:, l], AKTb=[self.AKTcb[l]], AV=lambda n, l=l: self.AVc[:, l, n], AVb=[self.AVcb[l]])
            if st >= 6 + 2 * l:
                self.phaseA(lat, l, l)
            if st >= 7 + 2 * l:
                self.phaseB(lat, l, c, last=(l == DEPTH - 1), b=b)
        if st < 99 and b == 0:
            self.dump("XC", self.XC[:].rearrange("p a b -> p (a b)"), self.XCb, [128, KC * LC], F32)
            self.dump("AKTc", self.AKTc[:].rearrange("p a b -> p (a b)"), self.AKTcb, [128, DEPTH * LC], BF16)
            self.dump("AVc", self.AVc[:].rearrange("p a b c -> p (a b c)"), self.AVcb, [128, DEPTH * 256], BF16)
            self.dump("R0", self.R0[:].rearrange("p a b c d -> p (a b c d)"), self.R0b[0] + self.R0b[1], [128, DEPTH * 512], F32)
            self.dump("RFc", self.RFc[:].rearrange("p a b c -> p (a b c)"), self.RFcb, [128, 512], BF16)
            self.dump("RBc", self.RBc[:].rearrange("p a b c -> p (a b c)"), self.RBcb, [128, 512], BF16)
            self.dump("UX", self.UX[:].rearrange("p a b -> p (a b)"), [self.UXb], [128, KC * TL], BF16)
            self.dump("KT", self.KT, self.KTb, [128, 2, TL], BF16)
            self.dump("VT", self.VT, self.VTb, [128, 4, 512], BF16)
            self.dump("YRT", self.YRT[:], self.YRTb, [128, 4, TL], BF16)
            self.dump("YAT", self.YAT[:], self.YATb, [128, 4, TL], BF16)
            self.dump("YPT", self.YPT[:], self.YPTb, [128, 4, TL], BF16)
            self.dump("XT", self.XT[:].rearrange("p a b -> p (a b)"), self.XTb, [128, KC * S], F32)
            self.dump("RB", self.RB[:].rearrange("p a b c -> p (a b c)"), self.RBb, [128, 4096], BF16)
            self.dump("AKT", self.AKT[:], self.AKTb, [128, S], BF16)

    def norm_tile(self, sq, l, ti, which):
        T, X, who = sq["T"], sq["X"], sq["who"]
        xb = sq["Xb"][ti]
        t0 = ti * T
        xs = X[:, :, t0:t0 + T]
        self.act(self.UX[:, :, 0:T], xs, AF.Square, R=[xb], W=[self.UXb])
        bt, bb = self.bank()
        for kc in range(KC):
            self.mm(bt[:, 0:T], self.CB[:, CB_ONES1024:CB_ONES1024 + 128], self.UX[:, kc, 0:T],
                    start=(kc == 0), stop=(kc == KC - 1), R=[self.UXb, self.CBb], W=[bb])
        self.rsqrt(self.RSTD[:, 0:T], bt[:, 0:T], R=[bb], W=[self.RSTDb])
        shc = 0 if which == 0 else 24
        for kc in range(KC):
            f, fb = self.f32r.next()
            self.stt("dve", f[:, 0:T], X[:, kc, t0:t0 + T], self.MODA[:, l, who, which * 8 + kc:which * 8 + kc + 1],
                     self.RSTD[:, 0:T], ALU.mult, ALU.mult, R=[xb, self.RSTDb, self.MODb], W=[fb])
            self.act(self.UX[:, kc, 0:T], f[:, 0:T], AF.Identity, bias=self.modcol(l, who, shc + kc), R=[fb, self.MODb], W=[self.UXb])

    def proj_fm(self, slab, sb_, kcn, tot, c0, T, m=128, rhs=None, rhsb=None, N=None):
        bt, bb = self.bank()
        sv = slab[:, 0:kcn * tot].rearrange("p (kc c) -> p kc c", kc=kcn)
        if rhs is None:
            rhs = lambda kc: self.UX[:, kc, 0:T]
            rhsb = [self.UXb]
        N = N or T
        for kc in range(kcn):
            self.mm(bt[0:m, 0:N], sv[:, kc, c0:c0 + m], rhs(kc), start=(kc == 0), stop=(kc == kcn - 1), R=[sb_] + rhsb, W=[bb])
        return bt, bb

    def load_tab(self, which, t0, T):
        self.dma("sp", self.TAB[:, :, 0:T], self.rope_d[2 * which:2 * which + 2, :, t0:t0 + T].rearrange("a p t -> p a t"),
                 self.TABds, W=[self.TABb])

    def rope_from(self, src, srcb, pm_col, out, outb, T, extraR=()):
        bt, bb = self.bank()
        self.mm(bt[:, 0:T], self.CB[:, pm_col:pm_col + 128], src, R=[srcb, self.CBb], W=[bb])
        f1, f1b = self.f32r.next()
        self.tt("dve", f1[:, 0:T], src, self.TAB[:, 0, 0:T], ALU.mult, R=[srcb, self.TABb], W=[f1b])
        f2, f2b = self.f32r.next()
        self.tt("dve", f2[:, 0:T], bt[:, 0:T], self.TAB[:, 1, 0:T], ALU.mult, R=[bb, self.TABb], W=[f2b])
        self.tt("pool", out, f1[:, 0:T], f2[:, 0:T], ALU.add, R=[f1b, f2b], W=[outb])

    def ret_k_chunk(self, sq, slab, sb_, c0, c, T):
        bt, bb = self.proj_fm(slab, sb_, KC, 512, c0 + c * 128, T)
        if sq["rope"]:
            x, xb = self.b16r.next()
            self.act(x[:, 0:T], bt[:, 0:T], AF.Copy, R=[bb], W=[xb])
            self.rope_from(x[:, 0:T], xb, CB_PMR, self.KT[:, c, 0:T], self.KTb[c], T)
        else:
            self.act(self.KT[:, c, 0:T], bt[:, 0:T], AF.Copy, R=[bb], W=[self.KTb[c]])

    def v_chunks(self, slab, sb_, T, c0=0, tot=512):
        sv = slab[:, 0:KC * tot].rearrange("p (kc c) -> p kc c", kc=KC)
        for j in range(T // 128):
            bt, bb = self.bank()
            for kc in range(KC):
                self.mm(bt[:, 0:512], self.UX[:, kc, j * 128:(j + 1) * 128], sv[:, kc, c0:c0 + 512],
                        start=(kc == 0), stop=(kc == KC - 1), R=[sb_, self.UXb], W=[bb])
            self.cp("dve" if j % 2 == 0 else "pool" if False else "dve", self.VT[:, j, :], bt[:, 0:512], R=[bb], W=[self.VTb[j]])

    def phaseA(self, sq, l, r0l):
        CUT = CUT0 if (not KSEQ or KSEQ == sq["name"]) else 99.0
        self.layer_tables(l)
        T, nt = sq["T"], sq["nt"]
        nch = T // 128
        ntot = nt * nch
        for d in range(2):
            if r0l is None:
                self.S.add("pool", lambda e, d=d: e.memset(self.ST[:, d, 0], 0.0), (), [self.STb[d][0]])
            else:
                self.cp("pool", self.ST[:, d, 0], self.R0[:, r0l, d], R=[self.R0b[r0l][d]], W=[self.STb[d][0]])
        pp = 0
        for ti in range(nt):
            t0 = ti * T
            self.norm_tile(sq, l, ti, 0)
            if CUT <= 1:
                return
            hb = sq["HALOb"][ti]
            self.cp("pool", sq["HALO"][:, ti, 0], self.UX[:, :, 0:8], R=[self.UXb], W=[hb])
            self.cp("pool", sq["HALO"][:, ti, 1], self.UX[:, :, T - 8:T], R=[self.UXb], W=[hb])
            slab, sb_ = self.load_slab(l, self.SL_A0, 4096)
            if sq["rope"]:
                self.load_tab(0, t0, T)
            if CUT <= 2:
                return
            for c in range(2):
                self.ret_k_chunk(sq, slab, sb_, 0, c, T)
            if CUT <= 3:
                return
            bt, bb = self.proj_fm(slab, sb_, KC, 512, 256, T)
            zc, zcb = self.f32r.next()
            self.cp("dve", zc[:, 0:T], bt[:, 0:T], R=[bb], W=[zcb])
            x, xb = self.b16r.next()
            self.act(x[:, 0:T], zc[:, 0:T], AF.Square, R=[zcb], W=[xb])
            b2, b2b = self.bank()
            self.mm(b2[:, 0:T], self.CB[:, CB_BD64:CB_BD64 + 128], x[:, 0:T], R=[xb, self.CBb], W=[b2b])
            rs, rsb = self.f32r.next()
            self.rsqrt(rs[:, 0:T], b2[:, 0:T], R=[b2b], W=[rsb])
            akb = sq["AKTb"][ti]
            if sq["rope"]:
                zn, znb = self.b16r.next()
                self.stt("dve", zn[:, 0:T], zc[:, 0:T], self.prm(self.P_KNW + l), rs[:, 0:T], ALU.mult, ALU.mult, R=[zcb, rsb, self.PRMb], W=[znb])
                if CUT <= 3.2:
                    return
                self.load_tab(1, t0, T)
                if CUT <= 3.3:
                    return
                self.rope_from(zn[:, 0:T], znb, CB_PMA, sq["AKT"][:, t0:t0 + T], akb, T)
            else:
                self.stt("dve", sq["AKT"][:, t0:t0 + T], zc[:, 0:T], self.prm(self.P_KNW + l), rs[:, 0:T], ALU.mult, ALU.mult, R=[zcb, rsb, self.PRMb], W=[akb])
            if CUT <= 4:
                return
            sv = slab[:, 0:4096].rearrange("p (kc c) -> p kc c", kc=KC)
            for j in range(nch):
                b3, b3b = self.bank()
                for kc in range(KC):
                    self.mm(b3[:, 0:128], self.UX[:, kc, j * 128:(j + 1) * 128], sv[:, kc, 384:512],
                            start=(kc == 0), stop=(kc == KC - 1), R=[sb_, self.UXb], W=[b3b])
                self.act(sq["AV"](ti * nch + j), b3[:, 0:128], AF.Copy, R=[b3b], W=[sq["AVb"][ti]])
            if CUT <= 5:
                return
            slab2, sb2 = self.load_slab(l, self.SL_RV, 4096)
            self.v_chunks(slab2, sb2, T)
            if CUT <= 6:
                return
            for j in range(nch):
                n = ti * nch + j
                kz, kzb = self.b16r.next()
                for c in range(2):
                    bt, bb = self.bank()
                    btb = bt.bitcast(BF16) if hasattr(bt, "bitcast") else bt
                    pt = btb[:, 0:128]
                    self.S.add("pe", lambda e, o=pt, i=self.KT[:, c, j * 128:(j + 1) * 128], idn=self.CB[:, CB_IDENT:CB_IDENT + 128]:
                               e.transpose(o, i, idn), [self.KTb[c], self.CBb], [bb])
                    for s in range(2):
                        h = 2 * c + s
                        for d in range(2):
                            dst = kz[:, d * 256 + h * 64:d * 256 + h * 64 + 64]
                            if c == 0:
                                self.act(dst, pt[:, s * 64:(s + 1) * 64], AF.Copy, scale=self.ZETA[:, d, h:h + 1], R=[bb, self.LTb], W=[kzb])
                            else:
                                self.ts("dve", dst, pt[:, s * 64:(s + 1) * 64], self.ZETA[:, d, h:h + 1], None, ALU.mult, R=[bb, self.LTb], W=[kzb])
                if CUT <= 7:
                    return
                ub, ubb = self.bank()
                for d in range(2):
                    for h in range(4):
                        c, s = h // 2, h % 2
                        self.mm(ub[64 * s:64 * s + 64, (d * 2 + c) * 128:(d * 2 + c + 1) * 128],
                                kz[:, d * 256 + h * 64:d * 256 + h * 64 + 64], self.VT[:, j, h * 128:(h + 1) * 128],
                                tp=(0, 64 * s), R=[kzb, self.VTb[j]], W=[ubb])
                if CUT <= 8:
                    return
                self.cp("pool", sq["RF"](n), self.ST[:, 0, pp], R=[self.STb[0][pp]], W=[sq["RFb"][n]])
                if CUT <= 8.1:
                    return
                for c in range(2):
                    self.stt("dve", self.ST[:, 0, 1 - pp, c], self.ST[:, 0, pp, c], self.prm(self.P_GC + l * 4 + c),
                             ub[:, c * 128:(c + 1) * 128], ALU.mult, ALU.add, R=[self.STb[0][pp], ubb, self.PRMb], W=[self.STb[0][1 - pp]])
                if CUT <= 8.2:
                    return
                self.cp("dve", sq["RB"](n), ub[:, 256:512].rearrange("p (c e) -> p c e", c=2), R=[ubb], W=[sq["RBb"][n]])
                pp = 1 - pp
                if CUT <= 8.3:
                    return
        ppf = pp
        if CUT <= 9:
            return
        pb = 0
        for n in range(ntot - 1, -1, -1):
            f, fb = self.f32r.next()
            self.cp("pool", f[:, 0:256].rearrange("p (c e) -> p c e", c=2), sq["RB"](n), R=[sq["RBb"][n]], W=[fb])
            self.cp("pool", sq["RB"](n), self.ST[:, 1, pb], R=[self.STb[1][pb], fb], W=[sq["RBb"][n]])
            for c in range(2):
                self.stt("dve", self.ST[:, 1, 1 - pb, c], self.ST[:, 1, pb, c], self.prm(self.P_GC + l * 4 + 2 + c),
                         f[:, c * 128:(c + 1) * 128], ALU.mult, ALU.add, R=[self.STb[1][pb], fb, self.PRMb], W=[self.STb[1][1 - pb]])
            pb = 1 - pb
        if r0l is None:
            self.cp("pool", self.R0[:, l, 0], self.ST[:, 0, ppf], R=[self.STb[0][ppf]], W=[self.R0b[l][0]])
            self.cp("pool", self.R0[:, l, 1], self.ST[:, 1, pb], R=[self.STb[1][pb]], W=[self.R0b[l][1]])

    def phaseB(self, sq, l, cx, last, b=0):
        T, nt, who = sq["T"], sq["nt"], sq["who"]
        nch = T // 128
        CUT = CUT0 if (not KSEQ or KSEQ == sq["name"]) else 99.0
        self.layer_tables(l)
        for ti in range(nt):
            t0 = ti * T
            xb = sq["Xb"][ti]
            X = sq["X"]
            self.norm_tile(sq, l, ti, 0)
            slab, sb_ = self.load_slab(l, self.SL_B0, 4096)
            if sq["rope"]:
                self.load_tab(0, t0, T)
            for c in range(2):
                bt, bb = self.proj_fm(slab, sb_, KC, 512, c * 128, T)
                if sq["rope"]:
                    x, xbf = self.b16r.next()
                    self.act(x[:, 0:T], bt[:, 0:T], AF.Copy, scale=0.125, R=[bb], W=[xbf])
                    self.rope_from(x[:, 0:T], xbf, CB_PMR, self.QT[:, c, 0:T], self.QTb[c], T)
                else:
                    self.act(self.QT[:, c, 0:T], bt[:, 0:T], AF.Copy, scale=0.125, R=[bb], W=[self.QTb[c]])
                qv = self.QT[:, c, 0:T].rearrange("p (j i) -> p j i", i=128)
                self.tt("pool", self.QXF[:, c, 0:T].rearrange("p (j i) -> p j i", i=128), qv, bc_mid(self.XI[:, 0, c, :], nch), ALU.mult,
                        R=[self.QTb[c], self.LTb], W=[self.QXFb[c]])
                self.tt("pool", self.QXB[:, c, 0:T].rearrange("p (j i) -> p j i", i=128), qv, bc_mid(self.XI[:, 1, c, :], nch), ALU.mult,
                        R=[self.QTb[c], self.LTb], W=[self.QXBb[c]])
            if CUT <= 11:
                return
            for c in range(2):
                self.ret_k_chunk(sq, slab, sb_, 256, c, T)
            slab, sb_ = self.load_slab(l, self.SL_RV, 4096)
            self.v_chunks(slab, sb_, T)
            slab, sb_ = self.load_slab(l, self.SL_RG, 4096)
            for h in range(4):
                bt, bb = self.proj_fm(slab, sb_, KC, 512, h * 128, T)
                self.act(self.SRG[:, h, 0:T], bt[:, 0:T], AF.Silu, R=[bb], W=[self.SRGb[h]])
            if CUT <= 12:
                return
            for j in range(nch):
                n = ti * nch + j
                js = slice(j * 128, (j + 1) * 128)
                sbk = [self.bank(), self.bank()]
                for h in range(4):
                    c, s = h // 2, h % 2
                    self.mm(sbk[s][0][:, c * 128:(c + 1) * 128], self.KT[64 * s:64 * s + 64, c, js], self.QT[64 * s:64 * s + 64, c, js],
                            tp=(64 * s, 0), R=[self.KTb[c], self.QTb[c]], W=[sbk[s][1]])
                p, pb_ = self.b16r.next()
                dtv = self.DT[:].rearrange("p (c s) i -> p c s i", s=2)
                pv = p[:, 0:512].rearrange("p (c s i) -> p c s i", c=2, s=2)
                for s in range(2):
                    self.tt("dve", pv[:, :, s, :], sbk[s][0][:, 0:256].rearrange("p (c i) -> p c i", c=2), dtv[:, :, s, :], ALU.mult,
                            R=[sbk[s][1], self.LTb], W=[pb_])
                if CUT <= 12.1:
                    return
                yb, ybb = self.bank()
                for h in range(4):
                    c, s = h // 2, h % 2
                    o = yb[:, h * 128:(h + 1) * 128]
                    self.mm(o, self.VT[:, j, h * 128:(h + 1) * 128], p[:, h * 128:(h + 1) * 128], start=True, stop=False,
                            R=[self.VTb[j], pb_], W=[ybb])
                    self.mm(o, sq["RF"](n)[64 * s:64 * s + 64, c, :], self.QXF[64 * s:64 * s + 64, c, js], start=False, stop=False,
                            tp=(64 * s, 0), R=[sq["RFb"][n], self.QXFb[c]], W=[ybb])
                    self.mm(o, sq["RB"](n)[64 * s:64 * s + 64, c, :], self.QXB[64 * s:64 * s + 64, c, js], start=False, stop=True,
                            tp=(64 * s, 0), R=[sq["RBb"][n], self.QXBb[c]], W=[ybb])
                if CUT <= 12.2:
                    return
                yraw, yrawb = self.f32r.next()
                self.cp("dve", yraw[:, 0:512], yb[:, 0:512], R=[ybb], W=[yrawb])
                x, xbf = self.b16r.next()
                self.act(x[:, 0:512], yraw[:, 0:512], AF.Square, R=[yrawb], W=[xbf])
                mb, mbb = self.bank()
                self.mm(mb[:, 0:512], self.CB[:, CB_ONES128:CB_ONES128 + 128], x[:, 0:512], R=[xbf, self.CBb], W=[mbb])
                rs, rsb = self.f32r.next()
                self.rsqrt(rs[:, 0:512], mb[:, 0:512], R=[mbb], W=[rsb])
                if CUT <= 12.3:
                    return
                yn, ynb = self.f32r.next()
                self.tt("dve", yn[:, 0:512], yraw[:, 0:512], rs[:, 0:512], ALU.mult, R=[yrawb, rsb], W=[ynb])
                if CUT <= 12.4:
                    return
                self.tt("pool", self.YRT[:, :, js], yn[:, 0:512].rearrange("p (h i) -> p h i", h=4), self.SRG[:, :, js], ALU.mult,
                        R=[ynb] + self.SRGb, W=self.YRTb)
            if CUT <= 13:
                return
            slab, sb_ = self.load_slab(l, self.SL_AQ, 4096)
            for cq in range(4):
                bt, bb = self.proj_fm(slab, sb_, KC, 512, cq * 128, T)
                zc, zcb = self.f32r.next()
                self.cp("dve", zc[:, 0:T], bt[:, 0:T], R=[bb], W=[zcb])
                x, xbf = self.b16r.next()
                self.act(x[:, 0:T], zc[:, 0:T], AF.Square, R=[zcb], W=[xbf])
                b2, b2b = self.bank()
                self.mm(b2[:, 0:T], self.CB[:, CB_BD64:CB_BD64 + 128], x[:, 0:T], R=[xbf, self.CBb], W=[b2b])
                rs, rsb = self.f32r.next()
                self.rsqrt(rs[:, 0:T], b2[:, 0:T], R=[b2b], W=[rsb])
                self.stt("dve", self.ZN[:, cq, 0:T], zc[:, 0:T], self.prm(self.P_QNW + l), rs[:, 0:T], ALU.mult, ALU.mult,
                         R=[zcb, rsb, self.PRMb], W=[self.ZNb[cq]])
            if sq["rope"]:
                self.load_tab(1, t0, T)
            for cp_ in range(4):
                s = cp_ % 2
                srcA, srcB = cp_ // 2, 2 + cp_ // 2
                bt, bb = self.bank()
                self.mm(bt[:, 0:T], self.CB[:, CB_SELA0 + s * 128:CB_SELA0 + (s + 1) * 128], self.ZN[:, srcA, 0:T], start=True, stop=False,
                        R=[self.ZNb[srcA], self.CBb], W=[bb])
                self.mm(bt[:, 0:T], self.CB[:, CB_SELB0 + s * 128:CB_SELB0 + (s + 1) * 128], self.ZN[:, srcB, 0:T], start=False, stop=True,
                        R=[self.ZNb[srcB], self.CBb], W=[bb])
                if sq["rope"]:
                    b2, b2b = self.bank()
                    self.mm(b2[:, 0:T], self.CB[:, CB_SWA0 + s * 128:CB_SWA0 + (s + 1) * 128], self.ZN[:, srcA, 0:T], start=True, stop=False,
                            R=[self.ZNb[srcA], self.CBb], W=[b2b])
                    self.mm(b2[:, 0:T], self.CB[:, CB_SWB0 + s * 128:CB_SWB0 + (s + 1) * 128], self.ZN[:, srcB, 0:T], start=False, stop=True,
                            R=[self.ZNb[srcB], self.CBb], W=[b2b])
                    f1, f1b = self.f32r.next()
                    self.tt("dve", f1[:, 0:T], bt[:, 0:T], self.TAB[:, 0, 0:T], ALU.mult, R=[bb, self.TABb], W=[f1b])
                    f2, f2b = self.f32r.next()
                    self.tt("dve", f2[:, 0:T], b2[:, 0:T], self.TAB[:, 1, 0:T], ALU.mult, R=[b2b, self.TABb], W=[f2b])
                    self.tt("pool", self.AQT[:, cp_, 0:T], f1[:, 0:T], f2[:, 0:T], ALU.add, R=[f1b, f2b], W=[self.AQTb[cp_]])
                else:
                    self.act(self.AQT[:, cp_, 0:T], bt[:, 0:T], AF.Copy, R=[bb], W=[self.AQTb[cp_]])
            if CUT <= 14:
                return
            nblk = sq["L"] // 128
            for j in range(nch):
                n = ti * nch + j
                js = slice(j * 128, (j + 1) * 128)
                blocks = []
                if sq["name"] == "lat":
                    for dn, mk in ((-1, CB_MASKP), (0, None), (1, CB_MASKN)):
                        nn = n + dn
                        if 0 <= nn < nblk:
                            blocks.append((sq["AKT"][:, nn * 128:(nn + 1) * 128], sq["AV"](nn),
                                           [sq["AKTb"][nn // nch], sq["AVb"][nn // nch]], mk))
                for cn in range(LC // 128):
                    blocks.append((cx["AKT"][:, cn * 128:(cn + 1) * 128], cx["AV"](cn), [cx["AKTb"][0], cx["AVb"][0]], None))
                obe = self.banks.next()
                dbe = self.banks.next()
                (ob, obb), (db, dbb) = obe, dbe
                lring = Ring([e for e in self.banks.items if e is not obe and e is not dbe])
                for k in range(2):
                    for bi, (kap, vap, kvb, mk) in enumerate(blocks):
                        lb, lbb = lring.next()
                        self.mm(lb[:, 0:512], kap[64 * k:64 * k + 64, :], self.AQT[64 * k:64 * k + 64, :, js], start=True, stop=(mk is None),
                                tp=(64 * k, 0), R=kvb + self.AQTb, W=[lbb])
                        if mk is not None:
                            self.mm(lb[:, 0:512], self.CB[:, mk:mk + 128], self.CB[:, CB_IDENT2:CB_IDENT2 + 512], start=False, stop=True,
                                    R=[self.CBb], W=[lbb])
                        pt, ptb = self.ptr.next()
                        self.act(pt[:, 0:512], lb[:, 0:512], AF.Exp, scale=0.125, R=[lbb], W=[ptb])
                        first, lastb = (bi == 0), (bi == len(blocks) - 1)
                        self.mm(ob[64 * k:64 * k + 64, 0:512], vap[:, 64 * k:64 * k + 64], pt[:, 0:512], start=first, stop=lastb,
                                tp=(0, 64 * k), R=kvb + [ptb], W=[obb])
                        self.mm(db[64 * k:64 * k + 64, 0:512], self.CB[:, CB_ONES:CB_ONES + 64], pt[:, 0:512], start=first, stop=lastb,
                                tp=(0, 64 * k), R=[ptb, self.CBb], W=[dbb])
                dn_, dnb = self.f32r.next()
                self.tt("dve", dn_[:, 0:512].rearrange("p (h i) -> p h i", h=4), db[:, 0:512].rearrange("p (h i) -> p h i", h=4),
                        bc_last(self.PRM[:, self.P_SINK + l * 4:self.P_SINK + l * 4 + 4], 128), ALU.add, R=[dbb, self.PRMb], W=[dnb])
                rd, rdb = self.f32r.next()
                self.recip(rd[:, 0:512], dn_[:, 0:512], R=[dnb], W=[rdb])
                self.tt("dve", self.YAT[:, :, js], ob[:, 0:512].rearrange("p (h i) -> p h i", h=4),
                        rd[:, 0:512].rearrange("p (h i) -> p h i", h=4), ALU.mult, R=[obb, rdb], W=self.YATb)
            if CUT <= 15:
                return
            slab, sb_ = self.load_slab(l, self.SL_PU, 4096)
            wins = (2, 4, 8, 16)
            hl = (sq["HALO"][:, ti - 1, 1], sq["HALOb"][ti - 1]) if ti > 0 else (self.ZHALO[:], self.ZHALOb)
            hr = (sq["HALO"][:, ti + 1, 0], sq["HALOb"][ti + 1]) if ti < nt - 1 else (self.ZHALO[:], self.ZHALOb)
            for g in range(4):
                w = wins[g]
                bt, bb = self.proj_fm(slab, sb_, KC, 512, g * 128, T)
                pu, pub = self.f32r.next()
                self.act(pu[:, 8:8 + T], bt[:, 0:T], AF.Copy, R=[bb], W=[pub])
                b2, b2b = self.bank()
                sv = slab[:, 0:4096].rearrange("p (kc c) -> p kc c", kc=KC)
                for side, (hap, hbuf) in enumerate((hl, hr)):
                    for kc in range(KC):
                        self.mm(b2[:, side * 8:side * 8 + 8], sv[:, kc, g * 128:(g + 1) * 128], hap[:, kc, :],
                                start=(kc == 0), stop=(kc == KC - 1), R=[sb_, hbuf], W=[b2b])
                self.cp("dve", pu[:, 0:8], b2[:, 0:8], R=[b2b], W=[pub])
                self.cp("dve", pu[:, 8 + T:16 + T], b2[:, 8:16], R=[b2b], W=[pub])
                cur, curb, span = pu, pub, 1
                lo = 8 - w // 2
                width = T + w
                while span < w // 2:
                    nx, nxb = self.f32r.next()
                    wd = T + w - 2 * span + 1 - 1
                    self.tt("pool", nx[:, lo:lo + wd], cur[:, lo:lo + wd], cur[:, lo + span:lo + span + wd], ALU.add, R=[curb], W=[nxb])
                    cur, curb, span = nx, nxb, span * 2
                sm, smb = self.f32r.next()
                self.tt("pool", sm[:, 0:T], cur[:, lo:lo + T], cur[:, lo + w // 2:lo + w // 2 + T], ALU.add, R=[curb], W=[smb])
                if ti == 0:
                    self.tt("pool", sm[:, 0:8], sm[:, 0:8], self.CF[:, CF_CORRL + g * 8:CF_CORRL + g * 8 + 8], ALU.mult, R=[smb, self.CFb], W=[smb])
                if ti == nt - 1:
                    self.tt("pool", sm[:, T - 8:T], sm[:, T - 8:T], self.CF[:, CF_CORRR + g * 8:CF_CORRR + g * 8 + 8], ALU.mult, R=[smb, self.CFb], W=[smb])
                df, dfb = self.b16r.next()
                self.stt("dve", df[:, 0:T], sm[:, 0:T], 1.0 / w, pu[:, 8:8 + T], ALU.mult, ALU.subtract, R=[smb, pub], W=[dfb])
                b3, b3b = self.bank()
                self.mm(b3[:, 0:T], self.PW[:, l, g, :], df[:, 0:T], R=[dfb, self.CBb], W=[b3b])
                self.act(self.YPT[:, g, 0:T], b3[:, 0:T], AF.Copy, scale=self.prm(self.P_PSC + l * 4 + g), R=[b3b, self.PRMb], W=[self.YPTb[g]])
            if CUT <= 16:
                return
            ysrc = ((self.YRT, self.YRTb), (self.YPT, self.YPTb), (self.YAT, self.YATb))
            for dm in range(8):
                gs, gsb = self.load_slab(l, self.SL_G + dm, KC * 384)
                bs, bsb = self.load_slab(l, self.SL_BO + dm, 1536)
                acc, accb = self.f32r.next()
                for x in range(3):
                    gb, gbb = self.proj_fm(gs, gsb, KC, 384, x * 128, T)
                    sg, sgb = self.f32r.next()
                    self.act(sg[:, 0:T], gb[:, 0:T], AF.Sigmoid, R=[gbb], W=[sgb])
                    yt_, ytb_ = ysrc[x]
                    pb2, pbb = self.proj_fm(bs[:, x * 512:(x + 1) * 512], bsb, 4, 128, 0, T,
                                            rhs=lambda kc, yt_=yt_: yt_[:, kc, 0:T], rhsb=list(ytb_))
                    if x == 0:
                        self.tt("dve", acc[:, 0:T], sg[:, 0:T], pb2[:, 0:T], ALU.mult, R=[sgb, pbb], W=[accb])
                    else:
                        self.tt("dve", sg[:, 0:T], sg[:, 0:T], pb2[:, 0:T], ALU.mult, R=[sgb, pbb], W=[sgb])
                        if x == 1:
                            self.tt("pool", acc[:, 0:T], acc[:, 0:T], sg[:, 0:T], ALU.add, R=[sgb, accb], W=[accb])
                        else:
                            self.tt("pool", self.YT[:, dm, 0:T], acc[:, 0:T], sg[:, 0:T], ALU.add, R=[sgb, accb], W=[self.YTb[dm]])
            if CUT <= 17:
                return
            for q in range(2):
                slab, sb_ = self.load_slab(l, self.SL_WO + q, 4096)
                for dd in range(4):
                    dm = q * 4 + dd
                    bt, bb = self.proj_fm(slab, sb_, KC, 512, dd * 128, T, rhs=lambda kc: self.YT[:, kc, 0:T], rhsb=list(self.YTb))
                    self.stt("dve", X[:, dm, t0:t0 + T], bt[:, 0:T], self.modcol(l, who, 16 + dm), X[:, dm, t0:t0 + T], ALU.mult, ALU.add,
                             R=[bb, self.MODb, xb], W=[xb])
            if CUT <= 18:
                return
            self.norm_tile(sq, l, ti, 1)
            for hq in range(4):
                for i2 in range(2):
                    slab, sb_ = self.load_slab(l, self.SL_W1 + hq * 2 + i2, 4096)
                    for cc in range(4):
                        hc = i2 * 4 + cc
                        bt, bb = self.proj_fm(slab, sb_, KC, 512, cc * 128, T)
                        r, rb_ = self.f32r.next()
                        self.act(r[:, 0:T], bt[:, 0:T], AF.Relu, R=[bb], W=[rb_])
                        self.tt("pool", self.HT[:, hc, 0:T], r[:, 0:T], r[:, 0:T], ALU.mult, R=[rb_], W=[self.HTb[hc]])
                for q in range(2):
                    slab, sb_ = self.load_slab(l, self.SL_W2 + hq * 2 + q, 4096)
                    for dd in range(4):
                        dm = q * 4 + dd
                        bt, bb = self.proj_fm(slab, sb_, KC, 512, dd * 128, T, rhs=lambda kc: self.HT[:, kc, 0:T], rhsb=list(self.HTb))
                        self.stt("dve", X[:, dm, t0:t0 + T], bt[:, 0:T], self.modcol(l, who, 40 + dm), X[:, dm, t0:t0 + T], ALU.mult, ALU.add,
                                 R=[bb, self.MODb, xb], W=[xb])
            if last:
                self.dma("sp", self.yT[b][:, t0:t0 + T].rearrange("(kc p) t -> p kc t", p=128), X[:, :, t0:t0 + T], self.outds,
                         R=[xb], W=[self.outb])


_PROG = None
_CONSTS = None


def kernel(**inputs):
    global _PROG, _CONSTS
    if _CONSTS is None:
        _CONSTS = _host_consts()
    cf, cb, rope = _CONSTS
    if _PROG is None:
        _PROG = Prog()
    prog = _PROG
    x = np.asarray(inputs["x"], np.float32)
    ctx = np.asarray(inputs["ctx"], np.float32)
    c = np.asarray(inputs["c"], np.float32)
    c_ctx = np.asarray(inputs["c_ctx"], np.float32)
    shared = {}
    for k in ("norm1_w", "norm2_w", "ada_w", "ada_b", "w_in", "pool_w", "pool_scale", "q_norm_w", "k_norm_w",
              "w_ret_out", "w_pool_out", "w_attn_out", "w_out", "w_mlp1", "w_mlp2"):
        shared[k] = np.ascontiguousarray(np.asarray(inputs[k], np.float32))
    shared["ret_decay"] = np.ascontiguousarray(np.asarray(inputs["ret_decay"], np.float32).reshape(-1))
    shared["attn_sink"] = np.ascontiguousarray(np.asarray(inputs["attn_sink"], np.float32).reshape(-1))
    shared["cstf"], shared["cstb"], shared["rope"] = cf, cb, rope
    in_maps = []
    for core in range(NCORES):
        bs = slice(core * NB, (core + 1) * NB)
        m = dict(shared)
        m["xT"] = np.ascontiguousarray(x[bs].transpose(0, 2, 1))
        m["cxT"] = np.ascontiguousarray(ctx[bs].transpose(0, 2, 1))
        m["cT"] = np.ascontiguousarray(np.stack([c[core * NB], c[core * NB + 1], c_ctx], axis=1))
        in_maps.append(m)
    res = run_bass_kernel_spmd(prog.nc, in_maps, core_ids=list(range(NCORES)))
    out = np.empty((NCORES * NB, S, D), np.float32)
    for core in range(NCORES):
        out[core * NB:(core + 1) * NB] = res.results[core]["yT"].transpose(0, 2, 1)
    return out
```
